# Optimizing a Trainium2 kernel written in Bass

```python
import math
import jax, jax.numpy as jnp
from jax import lax
import numpy as np


D_MODEL = 1024
BATCH = 16
SEQ = 2048
DEPTH = 2

HEAD_DIM = 64
ROPE_THETA = 10000.0
BLOCK = 128
NORM_EPS = 1e-6
NEG_INF = -1e30
DA_HEADS = D_MODEL // (2 * HEAD_DIM)
DA_QK = DA_HEADS * 2 * HEAD_DIM
DA_V = DA_HEADS * 2 * HEAD_DIM
WINDOW = 128
SW_HEADS = D_MODEL // HEAD_DIM
SW_KV_HEADS = 4
SW_GROUP = SW_HEADS // SW_KV_HEADS
FFN_HIDDEN = ((8 * D_MODEL // 3 + 255) // 256) * 256
N_A = DEPTH // 2
N_B = DEPTH - N_A

kernel_name = 'yoco_diffattn_swa_sink_hybrid'


def rms_norm(x, g):
    xf = x.astype(jnp.float32)
    y = xf * lax.rsqrt(jnp.mean(xf * xf, axis=-1, keepdims=True) + NORM_EPS)
    return (y * g.astype(jnp.float32)).astype(x.dtype)


def rope_tables(positions):
    inv = 1.0 / (ROPE_THETA ** (jnp.arange(0, HEAD_DIM, 2, dtype=jnp.float32) / HEAD_DIM))
    ang = positions.astype(jnp.float32)[..., None] * inv
    return jnp.cos(ang), jnp.sin(ang)


def apply_rope(x, cos, sin):
    extra = x.ndim - 3
    shp = cos.shape[:2] + (1,) * extra + cos.shape[-1:]
    c = cos.reshape(shp)
    s = sin.reshape(shp)
    x1, x2 = jnp.split(x.astype(jnp.float32), 2, axis=-1)
    return jnp.concatenate([x1 * c - x2 * s, x2 * c + x1 * s], axis=-1).astype(x.dtype)


def swiglu(h, w_gate_up, w_down):
    g, u = jnp.split(h @ w_gate_up, 2, axis=-1)
    return (jax.nn.silu(g) * u) @ w_down


def diff_attention(h, cos, sin, w_qkv, q_norm, k_norm, lam_p, subln, w_o, layer_idx):
    B, S, _ = h.shape
    lambda_init = 0.8 - 0.6 * math.exp(-0.3 * layer_idx)
    q, k, v = jnp.split(h @ w_qkv, [DA_QK, 2 * DA_QK], axis=-1)
    q = apply_rope(rms_norm(q.reshape(B, S, DA_HEADS, 2, HEAD_DIM), q_norm), cos, sin)
    k = apply_rope(rms_norm(k.reshape(B, S, DA_HEADS, 2, HEAD_DIM), k_norm), cos, sin)
    v = v.reshape(B, S, DA_HEADS, 2 * HEAD_DIM)
    lp = lam_p.astype(jnp.float32)
    lam = jnp.exp(jnp.sum(lp[0] * lp[1])) - jnp.exp(jnp.sum(lp[2] * lp[3])) + lambda_init
    scale = 1.0 / math.sqrt(HEAD_DIM)
    nb = S // BLOCK
    qb = jnp.moveaxis(q.reshape(B, nb, BLOCK, DA_HEADS, 2, HEAD_DIM), 1, 0)
    kpos = jnp.arange(S)

    def one_block(args):
        qblk, i = args
        s = jnp.einsum('bqhcd,bkhcd->bhcqk', qblk, k).astype(jnp.float32) * scale
        qpos = i * BLOCK + jnp.arange(BLOCK)
        causal = kpos[None, :] <= qpos[:, None]
        p = jax.nn.softmax(jnp.where(causal, s, NEG_INF), axis=-1)
        a = p[:, :, 0] - lam * p[:, :, 1]
        return jnp.einsum('bhqk,bkhe->bqhe', a.astype(v.dtype), v)

    o = lax.map(one_block, (qb, jnp.arange(nb)))
    o = jnp.moveaxis(o, 0, 1).reshape(B, S, DA_HEADS, 2 * HEAD_DIM)
    o = rms_norm(o, subln) * (1.0 - lambda_init)
    return o.reshape(B, S, DA_V) @ w_o


def shared_kv(x, cos, sin, kv_norm, w_kv, k_norm):
    B, S, _ = x.shape
    k, v = jnp.split(rms_norm(x, kv_norm) @ w_kv, 2, axis=-1)
    k = apply_rope(rms_norm(k.reshape(B, S, SW_KV_HEADS, HEAD_DIM), k_norm), cos, sin)
    v = v.reshape(B, S, SW_KV_HEADS, HEAD_DIM)
    nb = S // WINDOW

    def band(t):
        tp = jnp.pad(t, ((0, 0), (WINDOW, 0), (0, 0), (0, 0))).reshape(B, nb + 1, WINDOW, SW_KV_HEADS, HEAD_DIM)
        return jnp.concatenate([tp[:, :-1], tp[:, 1:]], axis=2)

    return band(k), band(v)


def swa_sink_attention(h, cos, sin, w_q, q_norm, sinks, w_o, kw, vw):
    B, S, _ = h.shape
    nb = S // WINDOW
    q = (h @ w_q).reshape(B, S, SW_KV_HEADS, SW_GROUP, HEAD_DIM)
    q = apply_rope(rms_norm(q, q_norm), cos, sin)
    qb = q.reshape(B, nb, WINDOW, SW_KV_HEADS, SW_GROUP, HEAD_DIM)
    s = jnp.einsum('bnqhgd,bnkhd->bnhgqk', qb, kw).astype(jnp.float32) * (1.0 / math.sqrt(HEAD_DIM))
    qi = jnp.arange(WINDOW)[:, None]
    kj = jnp.arange(2 * WINDOW)[None, :]
    in_band = (kj > qi) & (kj <= qi + WINDOW)
    valid = in_band[None] & ((jnp.arange(nb)[:, None, None] > 0) | (kj >= WINDOW)[None])
    s = jnp.where(valid[None, :, None, None], s, NEG_INF)
    sink = sinks.astype(jnp.float32).reshape(SW_KV_HEADS, SW_GROUP)[None, None, :, :, None, None]
    m = jnp.maximum(jnp.max(s, axis=-1, keepdims=True), sink)
    e = jnp.exp(s - m)
    p = e / (jnp.sum(e, axis=-1, keepdims=True) + jnp.exp(sink - m))
    o = jnp.einsum('bnhgqk,bnkhd->bnqhgd', p.astype(vw.dtype), vw)
    return o.reshape(B, S, SW_HEADS * HEAD_DIM) @ w_o


def setup_inputs(seed: int = 0) -> dict:
    key = jax.random.key(seed)
    ks = jax.random.split(key, 24)
    f32 = jnp.float32

    def w(k, shape, fan_in):
        return jax.random.normal(k, shape, f32) * (fan_in ** -0.5)

    def gain(k, shape):
        return 1.0 + 0.02 * jax.random.normal(k, shape, f32)

    x = jax.random.normal(ks[0], (BATCH, SEQ, D_MODEL), f32)
    offset = jax.random.randint(ks[1], (BATCH, 1), 0, 4096, dtype=jnp.int32)
    positions = jnp.arange(SEQ, dtype=jnp.int32)[None, :] + offset
    return {
        'x': x,
        'positions': positions,
        'attn_norm': gain(ks[2], (DEPTH, D_MODEL)),
        'ffn_norm': gain(ks[3], (DEPTH, D_MODEL)),
        'w_gate_up': w(ks[4], (DEPTH, D_MODEL, 2 * FFN_HIDDEN), D_MODEL),
        'w_down': w(ks[5], (DEPTH, FFN_HIDDEN, D_MODEL), FFN_HIDDEN),
        'da_w_qkv': w(ks[6], (N_A, D_MODEL, 2 * DA_QK + DA_V), D_MODEL),
        'da_q_norm': gain(ks[7], (N_A, 2, HEAD_DIM)),
        'da_k_norm': gain(ks[8], (N_A, 2, HEAD_DIM)),
        'da_lambda': 0.1 * jax.random.normal(ks[9], (N_A, 4, HEAD_DIM), f32),
        'da_subln': gain(ks[10], (N_A, 2 * HEAD_DIM)),
        'da_w_o': w(ks[11], (N_A, DA_V, D_MODEL), DA_V),
        'kv_norm': gain(ks[12], (D_MODEL,)),
        'w_kv': w(ks[13], (D_MODEL, 2 * SW_KV_HEADS * HEAD_DIM), D_MODEL),
        'k_norm': gain(ks[14], (HEAD_DIM,)),
        'sw_w_q': w(ks[15], (N_B, D_MODEL, SW_HEADS * HEAD_DIM), D_MODEL),
        'sw_q_norm': gain(ks[16], (N_B, HEAD_DIM)),
        'sw_sinks': 0.5 * jax.random.normal(ks[17], (N_B, SW_HEADS), f32),
        'sw_w_o': w(ks[18], (N_B, SW_HEADS * HEAD_DIM, D_MODEL), SW_HEADS * HEAD_DIM),
    }


def reference(x, positions, attn_norm, ffn_norm, w_gate_up, w_down, da_w_qkv, da_q_norm, da_k_norm,
              da_lambda, da_subln, da_w_o, kv_norm, w_kv, k_norm, sw_w_q, sw_q_norm, sw_sinks, sw_w_o):
    cos, sin = rope_tables(positions)
    kw = vw = None
    for l in range(DEPTH):
        h = rms_norm(x, attn_norm[l])
        if l < N_A:
            x = x + diff_attention(h, cos, sin, da_w_qkv[l], da_q_norm[l], da_k_norm[l],
                                   da_lambda[l], da_subln[l], da_w_o[l], l)
        else:
            if l == N_A:
                kw, vw = shared_kv(x, cos, sin, kv_norm, w_kv, k_norm)
                h = rms_norm(x, attn_norm[l])
            j = l - N_A
            x = x + swa_sink_attention(h, cos, sin, sw_w_q[j], sw_q_norm[j], sw_sinks[j], sw_w_o[j], kw, vw)
        x = x + swiglu(rms_norm(x, ffn_norm[l]), w_gate_up[l], w_down[l])
    return x
```

```python
import math
from contextlib import ExitStack

import numpy as np
import concourse.bass as bass
import concourse.mybir as mybir
from concourse.bass_utils import run_bass_kernel_spmd

F32 = mybir.dt.float32
BF16 = mybir.dt.bfloat16
I32 = mybir.dt.int32
ALU = mybir.AluOpType
AF = mybir.ActivationFunctionType
AX = mybir.AxisListType

D = 1024
S_LEN = 2048
T = 512
NST = 4
HID = 2816
NHC = 22
EPS = 1e-6
NRING = 8
LAMBDA_INIT = 0.8 - 0.6 * math.exp(-0.3 * 0)
NEG = -30000.0
import os as _os
F_STORE_EARLY = _os.environ.get("F_STORE_EARLY", "0") == "1"
F_PREFILL = _os.environ.get("F_PREFILL", "1") == "1"
F_TRIG_LATE = _os.environ.get("F_TRIG_LATE", "1") == "1"

P_AN0, P_AN1, P_FN0, P_FN1, P_KVN = 0, 8, 16, 24, 32
P_QN0, P_KN0, P_SUB, P_KN1, P_QN1 = 40, 41, 42, 43, 44
P_SINK = 45
P_LAM = 61
P_INVF = 61 + 256
NPC = P_INVF + 1
C_ID, C_MCUR, C_MPREV, C_BONES, C_OMEAN, C_RPERM, C_M01CUR, C_M01PREV = 0, 128, 256, 384, 512, 640, 768, 896
NCC = 1024


class Buf:
    __slots__ = ("name", "w", "rs", "rdma")

    def __init__(self, name):
        self.name = name
        self.w = None
        self.rs = {}
        self.rdma = []


class Op:
    __slots__ = ("eng", "fn", "deps", "need_inc", "is_dma", "dmabuf", "sem", "val")


class Sched:
    def __init__(self, nc, es):
        self.nc = nc
        self.es = es
        self.ops = []
        self.engs = {"pe": nc.tensor, "act": nc.scalar, "dve": nc.vector, "pool": nc.gpsimd, "sp": nc.sync}

    def add(self, eng, fn, reads=(), writes=(), dma=False, dmabuf=None, holds=()):
        op = Op()
        op.eng = eng
        op.fn = fn
        op.is_dma = dma
        op.dmabuf = dmabuf
        op.need_inc = False
        op.sem = None
        op.val = 0
        deps = {}

        def dep(d, raw):
            if d is op:
                return
            if (not d.is_dma) and (not dma) and d.eng == eng and eng == "pe":
                return
            deps[id(d)] = d

        for b in reads:
            if b.w is not None:
                dep(b.w, True)
        for b in writes:
            if b.w is not None:
                dep(b.w, False)
            for r in b.rs.values():
                dep(r, False)
            for r in b.rdma:
                dep(r, False)
        for b in list(reads) + list(holds):
            if dma:
                b.rdma.append(op)
            else:
                b.rs[eng] = op
        for b in writes:
            b.w = op
            b.rs = {}
            b.rdma = []
        op.deps = list(deps.values())
        for d in op.deps:
            d.need_inc = True
        self.ops.append(op)
        return op

    def emit(self):
        nc = self.nc
        sems = {}
        cnt = {}
        waited = {e: {} for e in self.engs}

        def getsem(key):
            if key not in sems:
                sems[key] = self.es.enter_context(nc.semaphore("s%d" % len(sems)))
                cnt[key] = 0
            return sems[key]

        for op in self.ops:
            eo = self.engs[op.eng]
            for d in op.deps:
                k = id(d.sem)
                if waited[op.eng].get(k, 0) < d.val:
                    eo.wait_ge(d.sem, d.val)
                    waited[op.eng][k] = d.val
            inst = op.fn(eo) if op.fn is not None else None
            if op.is_dma:
                key = ("dma", id(op.dmabuf), op.eng == "pool")
                s = getsem(key)
                insts = inst if isinstance(inst, (list, tuple)) else [inst]
                for i_ in insts:
                    cnt[key] += 16
                    i_.then_inc(s, 16)
                op.sem, op.val = s, cnt[key]
            elif op.need_inc:
                key = ("eng", op.eng)
                s = getsem(key)
                cnt[key] += 1
                inst.then_inc(s, 1)
                op.sem, op.val = s, cnt[key]
        self.stats = (len(self.ops), len(sems), {k[1]: v for k, v in cnt.items() if k[0] == "eng"})


def slab_table():
    sl = []
    for n in range(12):
        sl.append(("qkv%d" % n, [("da_w_qkv", 0, 0, 8, 256 * n, 256, 0, 256)]))
    for n in range(4):
        sl.append(("wo0_%d" % n, [("da_w_o", 0, 0, 8, 256 * n, 256, 0, 256)]))
    for l in range(2):
        if l == 1:
            for g2 in range(2):
                pcs = []
                for gl in range(2):
                    for dup in range(2):
                        pcs.append(("w_kv", None, 0, 8, (2 * g2 + gl) * 64, 64, gl * 128 + dup * 64, 256))
                sl.append(("kd%d" % g2, pcs))
            sl.append(("v1", [("w_kv", None, 0, 8, 256, 256, 0, 256)]))
            for n in range(4):
                sl.append(("q1_%d" % n, [("sw_w_q", 0, 0, 8, 256 * n, 256, 0, 256)]))
            for n in range(4):
                sl.append(("wo1_%d" % n, [("sw_w_o", 0, 0, 8, 256 * n, 256, 0, 256)]))
        for i in range(NHC):
            sl.append(("gu%d_%d" % (l, i), [("w_gate_up", l, 0, 8, 128 * i, 128, 0, 256),
                                            ("w_gate_up", l, 0, 8, HID + 128 * i, 128, 128, 256)]))
        for o in range(8):
            for hf in range(2):
                sl.append(("dn%d_%d_%d" % (l, o, hf), [("w_down", l, 11 * hf, 11, 128 * o, 128, 0, 128)]))
    return sl


def build_program(nseq=2, layers=(0, 1), n_mt=None, prologue=True):
    nc = bass.Bass("TRN2", target_bir_lowering=False)
    es = ExitStack()
    NTOK = nseq * S_LEN
    dr = {}
    dr["x"] = nc.dram_tensor("x", [NTOK, D], F32, kind="ExternalInput").ap()
    dr["pos"] = nc.dram_tensor("pos", [nseq, S_LEN], I32, kind="ExternalInput").ap()
    dr["da_w_qkv"] = nc.dram_tensor("da_w_qkv", [1, D, 3072], F32, kind="ExternalInput").ap()
    dr["da_w_o"] = nc.dram_tensor("da_w_o", [1, D, D], F32, kind="ExternalInput").ap()
    dr["w_gate_up"] = nc.dram_tensor("w_gate_up", [2, D, 2 * HID], F32, kind="ExternalInput").ap()
    dr["w_down"] = nc.dram_tensor("w_down", [2, HID, D], F32, kind="ExternalInput").ap()
    dr["w_kv"] = nc.dram_tensor("w_kv", [D, 512], F32, kind="ExternalInput").ap()
    dr["sw_w_q"] = nc.dram_tensor("sw_w_q", [1, D, D], F32, kind="ExternalInput").ap()
    dr["sw_w_o"] = nc.dram_tensor("sw_w_o", [1, D, D], F32, kind="ExternalInput").ap()
    dr["params"] = nc.dram_tensor("params", [128, NPC], F32, kind="ExternalInput").ap()
    dr["consts"] = nc.dram_tensor("consts", [128, NCC], F32, kind="ExternalInput").ap()
    out_d = nc.dram_tensor("out", [NTOK, D], F32, kind="ExternalOutput").ap()
    slabs = slab_table()
    NSLAB = len(slabs)
    slab_id = {s[0]: i for i, s in enumerate(slabs)}
    wscr = nc.dram_tensor("wscr", [NSLAB, 128, 2048], BF16).ap()

    S = Sched(nc, es)

    def sb(name, shape, dt):
        return es.enter_context(nc.sbuf_tensor(name, shape, dt))

    KT0 = sb("KT0", [128, 8, S_LEN], BF16)
    V0 = sb("V0", [128, 16, 8, 130], BF16)
    xT = sb("xT", [128, 8, T], F32)
    actA = sb("actA", [128, 8, T], BF16)
    actB = sb("actB", [128, 8, T], BF16)
    arena = sb("arena", [128, NHC, T], BF16)
    sqb = [sb("sqb%d" % i, [128, T], BF16) for i in range(2)]
    rstd = [sb("rstd%d" % i, [128, T], F32) for i in range(2)]
    rstdN = sb("rstdN", [128, T], F32)
    qnb = [sb("qnb%d" % i, [128, T], BF16) for i in range(2)]
    t1 = [sb("t1_%d" % i, [128, T], F32) for i in range(2)]
    t2 = [sb("t2_%d" % i, [128, T], F32) for i in range(2)]
    cosT = sb("cosT", [128, T], F32)
    sinT = sb("sinT", [128, T], F32)
    ti = sb("ti", [128, T], I32)
    PT = [sb("PT%d" % i, [128, 1024], BF16) for i in range(3)]
    rs = [sb("rs%d" % i, [128, 4, 2], F32) for i in range(2)]
    rs2 = [sb("rs2_%d" % i, [128, 4], F32) for i in range(2)]
    et = [sb("et%d" % i, [128, 4, 128], F32) for i in range(2)]
    eo = [sb("eo%d" % i, [128, 4, 128], F32) for i in range(2)]
    ssq = [sb("ssq%d" % i, [128, 4], F32) for i in range(2)]
    sg = [sb("sg%d" % i, [128, T], BF16) for i in range(2)]
    ring = [sb("ring%d" % i, [128, 2048], BF16) for i in range(NRING)]
    xs = sb("xs", [128, D], F32)
    ta = xs[:, 0:512]
    tb = xs[:, 512:1024]
    K1T = sb("K1T", [128, 4, 640], BF16)
    V1 = sb("V1", [128, 5, 4, 66], BF16)
    den = [sb("den%d" % i, [128, 4], F32) for i in range(2)]
    esink = sb("esink", [128, 16], F32)
    cbf = sb("cbf", [128, NCC], BF16)
    identf = sb("identf", [128, 128], F32)
    prm = sb("prm", [128, NPC], F32)
    epsb = sb("epsb", [128, 1], F32)
    eps128 = sb("eps128", [128, 1], F32)
    lamw = sb("lamw", [128, 2, 64], F32)
    lam2 = sb("lam2", [128, 2], F32)
    neglam = sb("neglam", [128, 1], F32)
    subg = sb("subg", [128, 1], F32)
    ps = es.enter_context(nc.psum_tensor("ps", [128, 8 * 512], F32))
    SBUF_LEFT = nc.sbuf_bytes_remaining

    def bank(b, c0=0, n=512):
        return ps[:, b * 512 + c0: b * 512 + c0 + n]

    B_bank = [Buf("bank%d" % i) for i in range(8)]
    B_KT0 = [[Buf("KT0_%d_%d" % (h, m)) for m in range(4)] for h in range(8)]
    B_V0 = [[Buf("V0_%d_%d" % (j, hp)) for hp in range(4)] for j in range(16)]
    B_xT = [Buf("xT%d" % c) for c in range(8)]
    B_actA = [Buf("actA%d" % c) for c in range(8)]
    B_actB = [Buf("actB%d" % c) for c in range(8)]
    B_ar = [Buf("ar%d" % c) for c in range(NHC)]
    B_obf = [Buf("obf%d" % h) for h in range(8)]
    B_sqb = [Buf("sqb%d" % i) for i in range(2)]
    B_rstd = [Buf("rstd%d" % i) for i in range(2)]
    B_rstdN = Buf("rstdN")
    B_qnb = [Buf("qnb%d" % i) for i in range(2)]
    B_t1 = [Buf("t1%d" % i) for i in range(2)]
    B_t2 = [Buf("t2%d" % i) for i in range(2)]
    B_cos, B_sin, B_ti = Buf("cos"), Buf("sin"), Buf("ti")
    B_PT = [Buf("PT%d" % i) for i in range(3)]
    B_rs = [Buf("rs%d" % i) for i in range(2)]
    B_rs2 = [Buf("rs2%d" % i) for i in range(2)]
    B_et = [Buf("et%d" % i) for i in range(2)]
    B_eo = [Buf("eo%d" % i) for i in range(2)]
    B_ssq = [Buf("ssq%d" % i) for i in range(2)]
    B_sg = [Buf("sg%d" % i) for i in range(2)]
    B_ring = [Buf("ring%d" % i) for i in range(NRING)]
    B_xs0, B_xs1 = Buf("xs0"), Buf("xs1")
    B_ta, B_tb = B_xs0, B_xs1
    B_K1T = [Buf("K1T%d" % g) for g in range(4)]
    B_K1Tprev = [Buf("K1Tp%d" % g) for g in range(4)]
    B_V1 = [Buf("V1_%d" % b) for b in range(5)]
    B_den = [Buf("den%d" % i) for i in range(2)]
    B_const = Buf("const")
    B_wscr = [Buf("wscr%d" % i) for i in range(NSLAB)]
    B_out = []

    cnt = {"ring": 0, "sqb": 0, "rstd": 0, "qnb": 0, "t1": 0, "t2": 0, "PT": 0, "ep": 0, "sg": 0, "den": 0}

    def rot(key, n):
        v = cnt[key] % n
        cnt[key] += 1
        return v

    S.add("pool", lambda e: e.dma_start(out=cbf[:], in_=dr["consts"]), writes=[B_const], dma=True, dmabuf=B_const)
    B_c2 = Buf("c2")
    S.add("sp", lambda e: [e.dma_start(out=identf[:], in_=dr["consts"][:, C_ID:C_ID + 128]),
                           e.dma_start(out=prm[:], in_=dr["params"])], writes=[B_c2], dma=True, dmabuf=B_c2)
    B_c3 = Buf("c3")
    S.add("pool", lambda e: e.memset(epsb[:], EPS), writes=[B_c3])
    S.add("pool", lambda e: e.memset(eps128[:], 128.0 * EPS), writes=[B_c3])
    S.add("pool", lambda e: e.memset(V0[:, :, :, 128:130], 1.0), writes=[b for r in B_V0 for b in r])
    S.add("pool", lambda e: e.memset(V1[:, :, :, 64:66], 1.0), writes=B_V1)
    B_l = Buf("lam")
    lamv = prm[:, P_LAM:P_LAM + 256].rearrange("p (a b d) -> p a b d", a=2, b=2)
    S.add("dve", lambda e: e.tensor_tensor(out=lamw[:], in0=lamv[:, :, 0, :], in1=lamv[:, :, 1, :], op=ALU.mult), reads=[B_c2], writes=[B_l])
    S.add("dve", lambda e: e.tensor_reduce(out=lam2[:], in_=lamw[:], axis=AX.X, op=ALU.add), reads=[B_l], writes=[B_l])
    S.add("act", lambda e: e.activation(out=lam2[:], in_=lam2[:], func=AF.Exp), reads=[B_l], writes=[B_l])
    S.add("dve", lambda e: e.tensor_tensor(out=neglam[:], in0=lam2[:, 1:2], in1=lam2[:, 0:1], op=ALU.subtract), reads=[B_l], writes=[B_l])
    S.add("dve", lambda e: e.tensor_scalar(out=neglam[:], in0=neglam[:], scalar1=-LAMBDA_INIT, scalar2=None, op0=ALU.add), reads=[B_l], writes=[B_l])
    S.add("dve", lambda e: e.tensor_scalar(out=subg[:], in0=prm[:, P_SUB:P_SUB + 1], scalar1=(1.0 - LAMBDA_INIT) * math.sqrt(128.0), scalar2=None, op0=ALU.mult), reads=[B_c2], writes=[B_l])
    S.add("act", lambda e: e.activation(out=esink[:], in_=prm[:, P_SINK:P_SINK + 16], func=AF.Exp), reads=[B_c2], writes=[B_l])
    CONST = [B_const, B_c2, B_c3, B_l]

    def cc(off, n=128):
        return cbf[:, off:off + n]

    def src_ap(piece):
        key, l, kc0, nkc, c0, ncol, d0, dstride = piece
        w = dr[key]
        if l is not None:
            w = w[l]
        return w.rearrange("(kc p) n -> p kc n", p=128)[:, kc0:kc0 + nkc, c0:c0 + ncol]

    def dst_ap(slot, piece):
        key, l, kc0, nkc, c0, ncol, d0, dstride = piece
        v = ring[slot][:, 0:nkc * dstride].rearrange("p (kc n) -> p kc n", n=dstride)
        return v[:, :, d0:d0 + ncol]

    LOOK = 6

    class SlabStream:
        def __init__(self, order, n_tiles):
            self.order = order
            self.idx = {nm: i for i, nm in enumerate(order)}
            self.n = len(order)
            self.total = n_tiles * self.n
            self.emitted = 0
            self.slots = {}
            self.tile = 0

        def _emit(self, gi):
            t, k = divmod(gi, self.n)
            name = self.order[k]
            si = slab_id[name]
            slot = rot("ring", NRING)
            if t == 0:
                pieces = slabs[si][1]

                def ld(e):
                    return [e.dma_start(out=dst_ap(slot, p), in_=src_ap(p)) for p in pieces]

                S.add("pool", ld, writes=[B_ring[slot]], dma=True, dmabuf=B_ring[slot])
                S.add("sp", lambda e: e.dma_start(out=wscr[si], in_=ring[slot][:]),
                      reads=[B_ring[slot]], writes=[B_wscr[si]], dma=True, dmabuf=B_ring[slot])
            else:
                S.add("sp", lambda e: e.dma_start(out=ring[slot][:], in_=wscr[si]),
                      reads=[B_wscr[si]], writes=[B_ring[slot]], dma=True, dmabuf=B_ring[slot])
            self.slots[gi] = slot

        def get(self, name):
            gi = self.tile * self.n + self.idx[name]
            upto = min(self.total, gi + 1 + LOOK)
            while self.emitted < upto:
                self._emit(self.emitted)
                self.emitted += 1
            return self.slots[gi]

        fetch = get

    order = []
    if 0 in layers:
        order += ["qkv%d" % n for n in range(12)] + ["wo0_%d" % n for n in range(4)]
        order += ["gu0_%d" % i for i in range(NHC)] + ["dn0_%d_%d" % (o, hf) for o in range(8) for hf in range(2)]
    if 1 in layers:
        order += ["kd0", "kd1"] + ["q1_%d" % n for n in range(4)] + ["v1"] + ["wo1_%d" % n for n in range(4)]
        order += ["gu1_%d" % i for i in range(NHC)] + ["dn1_%d_%d" % (o, hf) for o in range(8) for hf in range(2)]
    SS = SlabStream(order, nseq * (S_LEN // T if n_mt is None else n_mt))

    def slab_view(slot, stride):
        return ring[slot][:].rearrange("p (kc n) -> p kc n", n=stride)

    def rms_stats(ssbank):
        for c in range(8):
            q = rot("sqb", 2)
            S.add("act", lambda e, c=c, q=q: e.activation(out=sqb[q][:], in_=xT[:, c, :], func=AF.Square),
                  reads=[B_xT[c]], writes=[B_sqb[q]])
            S.add("pe", lambda e, c=c, q=q: e.matmul(bank(ssbank), lhsT=cc(C_OMEAN), rhs=sqb[q][:], start=(c == 0), stop=(c == 7)),
                  reads=[B_sqb[q]] + CONST, writes=[B_bank[ssbank]])
        S.add("act", lambda e: e.activation(out=rstdN[:], in_=bank(ssbank), func=AF.Ln, bias=epsb[:, 0:1], scale=1.0),
              reads=[B_bank[ssbank]] + CONST, writes=[B_rstdN])
        S.add("act", lambda e: e.activation(out=rstdN[:], in_=rstdN[:], func=AF.Exp, scale=-0.5),
              reads=[B_rstdN], writes=[B_rstdN])
        return 0

    def apply_norm(r, gcol, dst, B_dst, defer=False):
        ops = []
        for c in range(8):
            def one(c=c):
                S.add("dve", lambda e: e.scalar_tensor_tensor(out=dst[:, c, :], in0=xT[:, c, :], scalar=prm[:, gcol + c:gcol + c + 1],
                                                              in1=rstdN[:], op0=ALU.mult, op1=ALU.mult),
                      reads=[B_xT[c], B_rstdN] + CONST, writes=[B_dst[c]])
            ops.append(one)
        if defer:
            return ops
        for o_ in ops:
            o_()

    def proj_fm(slot, colo, act, B_act, obank, stride=256, split=False):
        sv = slab_view(slot, stride)
        if split:
            for kc in range(8):
                S.add("pe", lambda e, kc=kc: e.matmul(bank(obank), lhsT=sv[:, kc, colo:colo + 128], rhs=act[:, kc, :], start=(kc == 0), stop=(kc == 7)),
                      reads=[B_ring[slot], B_act[kc]], writes=[B_bank[obank]])
            return

        def f(e):
            for kc in range(8):
                i_ = e.matmul(bank(obank), lhsT=sv[:, kc, colo:colo + 128], rhs=act[:, kc, :], start=(kc == 0), stop=(kc == 7))
            return i_

        S.add("pe", f, reads=[B_ring[slot]] + B_act, writes=[B_bank[obank]])

    def qk_pipeline(jobs, fillers=(), prefill=()):
        n = len(jobs)
        RAW, SSB, ROT = (0, 1, 2), (3, 4), (5, 6)
        stt = [dict() for _ in jobs]

        def A(i):
            name, colo, act, B_act_, gcol, dst, B_dst = jobs[i]
            slot = SS.get(name)
            rb = RAW[i % 3]
            proj_fm(slot, colo, act, B_act_, rb, split=(i == 0))
            q = rot("sqb", 2)
            S.add("act", lambda e: e.activation(out=sqb[q][:], in_=bank(rb), func=AF.Square), reads=[B_bank[rb]], writes=[B_sqb[q]])
            stt[i].update(rb=rb, q=q)

        def Bst(i):
            name, colo, act, B_act_, gcol, dst, B_dst = jobs[i]
            rb, q = stt[i]["rb"], stt[i]["q"]
            sbk = SSB[i % 2]
            S.add("pe", lambda e: e.matmul(bank(sbk), lhsT=cc(C_BONES), rhs=sqb[q][:], start=True, stop=True),
                  reads=[B_sqb[q]] + CONST, writes=[B_bank[sbk]])
            r = rot("rstd", 2)
            S.add("act", lambda e: e.activation(out=rstd[r][:], in_=bank(sbk), func=AF.Ln, bias=epsb[:, 0:1], scale=1.0),
                  reads=[B_bank[sbk]] + CONST, writes=[B_rstd[r]])
            S.add("act", lambda e: e.activation(out=rstd[r][:], in_=rstd[r][:], func=AF.Exp, scale=-0.5), reads=[B_rstd[r]], writes=[B_rstd[r]])
            nn = rot("qnb", 2)
            S.add("dve", lambda e: e.scalar_tensor_tensor(out=qnb[nn][:], in0=bank(rb), scalar=prm[:, gcol:gcol + 1], in1=rstd[r][:],
                                                          op0=ALU.mult, op1=ALU.mult),
                  reads=[B_bank[rb], B_rstd[r]] + CONST, writes=[B_qnb[nn]])
            stt[i].update(nn=nn)

        def Cst(i):
            name, colo, act, B_act_, gcol, dst, B_dst = jobs[i]
            nn = stt[i]["nn"]
            rtb = ROT[i % 2]
            S.add("pe", lambda e: e.matmul(bank(rtb), lhsT=cc(C_RPERM), rhs=qnb[nn][:], start=True, stop=True),
                  reads=[B_qnb[nn]] + CONST, writes=[B_bank[rtb]])
            a = rot("t1", 2)
            S.add("pool", lambda e: e.tensor_tensor(out=t1[a][:], in0=qnb[nn][:], in1=cosT[:], op=ALU.mult), reads=[B_qnb[nn], B_cos], writes=[B_t1[a]])
            b = rot("t2", 2)
            S.add("dve", lambda e: e.tensor_tensor(out=t2[b][:], in0=bank(rtb), in1=sinT[:], op=ALU.mult), reads=[B_bank[rtb], B_sin], writes=[B_t2[b]])
            S.add("pool" if i % 2 else "dve", lambda e: e.tensor_tensor(out=dst, in0=t1[a][:], in1=t2[b][:], op=ALU.add), reads=[B_t1[a], B_t2[b]], writes=B_dst)

        fillers = list(fillers)
        per = (len(fillers) + 2) // 3
        prefill = list(prefill)
        for step in range(n + 2):
            for _ in range(2):
                if prefill:
                    prefill.pop(0)()
            if step < n:
                A(step)
            if 0 <= step - 1 < n:
                Bst(step - 1)
            if 0 <= step - 2 < n:
                Cst(step - 2)
            if step >= n - 1:
                for _ in range(per):
                    if fillers:
                        fillers.pop(0)()
        while fillers:
            fillers.pop(0)()

    def residual_add(c, obank):
        S.add("dve", lambda e: e.tensor_tensor(out=xT[:, c, :], in0=xT[:, c, :], in1=bank(obank), op=ALU.add),
              reads=[B_xT[c], B_bank[obank]], writes=[B_xT[c]])

    nacc = {"pend": None}

    def norm_acc_chunk(c):
        q = rot("sqb", 2)
        S.add("act", lambda e: e.activation(out=sqb[q][:], in_=xT[:, c, :], func=AF.Square), reads=[B_xT[c]], writes=[B_sqb[q]])
        norm_acc_flush(False)
        nacc["pend"] = (c, q)

    def norm_acc_flush(last):
        if nacc["pend"] is None:
            return
        c, q = nacc["pend"]
        S.add("pe", lambda e: e.matmul(bank(7), lhsT=cc(C_OMEAN), rhs=sqb[q][:], start=(c == 0), stop=(c == 7)),
              reads=[B_sqb[q]] + CONST, writes=[B_bank[7]])
        nacc["pend"] = None

    def norm_acc_finish():
        norm_acc_flush(True)
        S.add("act", lambda e: e.activation(out=rstdN[:], in_=bank(7), func=AF.Ln, bias=epsb[:, 0:1], scale=1.0),
              reads=[B_bank[7]] + CONST, writes=[B_rstdN])
        S.add("act", lambda e: e.activation(out=rstdN[:], in_=rstdN[:], func=AF.Exp, scale=-0.5), reads=[B_rstdN], writes=[B_rstdN])
        return 0

    def trig_dve(seq, tok0):
        S.add("sp", lambda e: e.dma_start(out=ti[:], in_=dr["pos"][seq:seq + 1, tok0:tok0 + T].partition_broadcast(128)),
              writes=[B_ti], dma=True, dmabuf=B_ti)
        S.add("dve", lambda e: e.tensor_copy(out=ta[:], in_=ti[:]), reads=[B_ti], writes=[B_ta])
        S.add("dve", lambda e: e.tensor_scalar(out=ta[:], in0=ta[:], scalar1=prm[:, P_INVF:P_INVF + 1], scalar2=float(1.0 / (2 * np.pi)),
                                               op0=ALU.mult, op1=ALU.mult), reads=[B_ta] + CONST, writes=[B_ta])
        for which, shift, dstt, B_d in (("s", 0.0, sinT, B_sin), ("c", 0.25, cosT, B_cos)):
            S.add("dve", lambda e, shift=shift: e.tensor_scalar(out=tb[:], in0=ta[:], scalar1=shift, scalar2=None, op0=ALU.add),
                  reads=[B_ta], writes=[B_tb])
            S.add("dve", lambda e: e.tensor_copy(out=ti[:], in_=tb[:]), reads=[B_tb], writes=[B_ti])
            S.add("dve", lambda e, dstt=dstt: e.tensor_copy(out=dstt[:], in_=ti[:]), reads=[B_ti], writes=[B_d])
            S.add("dve", lambda e, dstt=dstt: e.tensor_tensor(out=dstt[:], in0=tb[:], in1=dstt[:], op=ALU.subtract), reads=[B_tb, B_d], writes=[B_d])
            S.add("dve", lambda e, dstt=dstt: e.tensor_single_scalar(out=tb[:], in_=dstt[:], scalar=0.5, op=ALU.is_gt), reads=[B_d], writes=[B_tb])
            S.add("dve", lambda e, dstt=dstt: e.tensor_tensor(out=dstt[:], in0=dstt[:], in1=tb[:], op=ALU.subtract), reads=[B_tb, B_d], writes=[B_d])
            S.add("dve", lambda e, dstt=dstt: e.tensor_single_scalar(out=tb[:], in_=dstt[:], scalar=-0.5, op=ALU.is_lt), reads=[B_d], writes=[B_tb])
            S.add("dve", lambda e, dstt=dstt: e.tensor_tensor(out=dstt[:], in0=dstt[:], in1=tb[:], op=ALU.add), reads=[B_tb, B_d], writes=[B_d])

    def trig_act():
        for dstt, B_d in ((sinT, B_sin), (cosT, B_cos)):
            S.add("act", lambda e, dstt=dstt: e.activation(out=dstt[:], in_=dstt[:], func=AF.Sin, scale=float(2 * np.pi)), reads=[B_d], writes=[B_d])

    def trig_tables(seq, tok0):
        trig_dve(seq, tok0)
        trig_act()

    def stage_halves(st):
        if st < 2:
            v = actA[:, 4 * st:4 * st + 4, :].rearrange("p a b -> p (a b)").bitcast(F32)
            return [(v[:, 0:512], B_actA[4 * st:4 * st + 2]), (v[:, 512:1024], B_actA[4 * st + 2:4 * st + 4])]
        tt, BB = (t1, B_t1) if st == 2 else (t2, B_t2)
        return [(tt[0][:], [BB[0]]), (tt[1][:], [BB[1]])]

    def issue_loads(g0):
        for st in range(NST):
            r0 = g0 + st * 128
            for half, (ap_, bufs) in enumerate(stage_halves(st)):
                S.add("sp", lambda e, r0=r0, half=half, ap_=ap_: e.dma_start(out=ap_, in_=dr["x"][r0:r0 + 128, half * 512:(half + 1) * 512]),
                      writes=bufs, dma=True, dmabuf=bufs[0])

    def load_x(g0):
        for half in range(2):
            for st in range(NST):
                ap_, bufs = stage_halves(st)[half]
                bk = (2 * st + half) % 4

                def tr(e, ap_=ap_, bk=bk):
                    for c4 in range(4):
                        i_ = e.transpose(out=bank(bk, c4 * 128, 128), in_=ap_[:, c4 * 128:(c4 + 1) * 128], identity=identf[:])
                    return i_

                S.add("pe", tr, reads=bufs + CONST, writes=[B_bank[bk]])
                if half == 0:
                    S.add("act", lambda e, half=half, bk=bk, st=st: e.activation(
                        out=xT[:, half * 4:half * 4 + 4, st * 128:(st + 1) * 128],
                        in_=bank(bk).rearrange("p (c t) -> p c t", c=4), func=AF.Copy),
                        reads=[B_bank[bk]], writes=B_xT[half * 4:half * 4 + 4])
                else:
                    S.add("dve", lambda e, half=half, bk=bk, st=st: e.tensor_copy(
                        out=xT[:, half * 4:half * 4 + 4, st * 128:(st + 1) * 128],
                        in_=bank(bk).rearrange("p (c t) -> p c t", c=4)),
                        reads=[B_bank[bk]], writes=B_xT[half * 4:half * 4 + 4])
            for c in range(half * 4, half * 4 + 4):
                norm_acc_chunk(c)
        return norm_acc_finish()

    stg = [(xs[:, 0:512], B_xs0), (xs[:, 512:1024], B_xs1), (PT[0][:].bitcast(F32), B_PT[0]), (PT[1][:].bitcast(F32), B_PT[1])]
    stq = {"q": 0}

    def store_groups(g0, half):
        for st in range(NST):
            r0 = g0 + st * 128
            bk = (0, 1, 4, 5)[stq["q"] % 4]
            o_ap, o_b = stg[stq["q"] % 4]
            stq["q"] += 1

            def tr(e, half=half, bk=bk, st=st):
                for c4 in range(4):
                    c = half * 4 + c4
                    i_ = e.transpose(out=bank(bk, c4 * 128, 128), in_=xT[:, c, st * 128:(st + 1) * 128], identity=identf[:])
                return i_

            S.add("pe", tr, reads=B_xT[half * 4:half * 4 + 4] + CONST, writes=[B_bank[bk]])
            S.add("dve", lambda e, bk=bk, o_ap=o_ap: e.tensor_copy(out=o_ap, in_=bank(bk)), reads=[B_bank[bk]], writes=[o_b])
            bo = Buf("out")
            B_out.append(bo)
            S.add("sp", lambda e, r0=r0, half=half, o_ap=o_ap: e.dma_start(out=out_d[r0:r0 + 128, half * 512:(half + 1) * 512], in_=o_ap),
                  reads=[o_b], writes=[bo], dma=True, dmabuf=o_b)

    def transposes_to(act, B_act, scale_ap):
        for st in range(NST):
            bk = st % 2
            pb = bank(bk).bitcast(BF16)

            def tr(e, st=st, pb=pb):
                for c in range(8):
                    i_ = e.transpose(out=pb[:, c * 128:(c + 1) * 128], in_=arena[:, 2 * st + c // 4, (c % 4) * 128:(c % 4) * 128 + 128],
                                     identity=cc(C_ID))
                return i_

            S.add("pe", tr, reads=[B_ar[2 * st], B_ar[2 * st + 1]] + CONST, writes=[B_bank[bk]])
            if st % 2 == 1:
                if scale_ap is not None:
                    S.add("dve", lambda e, st=st, pb=pb: e.tensor_scalar(out=act[:, :, st * 128:(st + 1) * 128], in0=pb.rearrange("p (c t) -> p c t", c=8),
                                                                        scalar1=scale_ap, scalar2=None, op0=ALU.mult), reads=[B_bank[bk]] + CONST, writes=B_act)
                else:
                    S.add("dve", lambda e, st=st, pb=pb: e.tensor_copy(out=act[:, :, st * 128:(st + 1) * 128], in_=pb.rearrange("p (c t) -> p c t", c=8)),
                          reads=[B_bank[bk]], writes=B_act)
            elif scale_ap is not None:
                S.add("act", lambda e, st=st, pb=pb: e.activation(out=act[:, :, st * 128:(st + 1) * 128], in_=pb.rearrange("p (c t) -> p c t", c=8),
                                                                 func=AF.Copy, scale=scale_ap), reads=[B_bank[bk]] + CONST, writes=B_act)
            else:
                S.add("act", lambda e, st=st, pb=pb: e.activation(out=act[:, :, st * 128:(st + 1) * 128], in_=pb.rearrange("p (c t) -> p c t", c=8),
                                                                 func=AF.Copy), reads=[B_bank[bk]], writes=B_act)

    def out_proj(prefix, act, B_act, split_first=False):
        for n in range(4):
            slot = SS.fetch("%s_%d" % (prefix, n))
            for cl in range(2):
                o = 2 * n + cl
                ob = 2 + (o % 2)
                proj_fm(slot, cl * 128, act, B_act, ob, split=(split_first and o == 0))
                residual_add(o, ob)
                norm_acc_chunk(o)
        return norm_acc_finish()

    def ffn(l, gcol, r, hook2=None, acc=False, store_g0=None):
        apply_norm(r, gcol, actB, B_actB)
        if hook2 is not None:
            hook2[0]()
        for i in range(NHC):
            slot = SS.fetch("gu%d_%d" % (l, i))
            gb = 4 + (i % 2)
            ub = 6 + (i % 2)
            proj_fm(slot, 0, actB, B_actB, gb, split=(i == 0))
            proj_fm(slot, 128, actB, B_actB, ub)
            s_ = rot("sg", 2)
            S.add("act", lambda e, gb=gb, s_=s_: e.activation(out=sg[s_][:], in_=bank(gb), func=AF.Silu), reads=[B_bank[gb]], writes=[B_sg[s_]])
            S.add("dve", lambda e, ub=ub, s_=s_, i=i: e.tensor_tensor(out=arena[:, i, :], in0=sg[s_][:], in1=bank(ub), op=ALU.mult),
                  reads=[B_sg[s_], B_bank[ub]], writes=[B_ar[i]])
            if i == (8 if F_TRIG_LATE else 0) and hook2 is not None:
                hook2[1]()
        for o in range(8):
            slots = [SS.fetch("dn%d_%d_%d" % (l, o, hf)) for hf in range(2)]
            ob = 2 + (o % 2)

            def f(e, slots=slots, ob=ob):
                for kc in range(NHC):
                    sv = slab_view(slots[kc // 11], 128)
                    i_ = e.matmul(bank(ob), lhsT=sv[:, kc % 11, 0:128], rhs=arena[:, kc, :], start=(kc == 0), stop=(kc == NHC - 1))
                return i_

            S.add("pe", f, reads=[B_ring[slots[0]], B_ring[slots[1]]] + B_ar, writes=[B_bank[ob]])
            residual_add(o, ob)
            if acc:
                norm_acc_chunk(o)
            if store_g0 is not None and o == 4 and F_STORE_EARLY:
                store_groups(store_g0, 0)
        if store_g0 is not None:
            if not F_STORE_EARLY:
                store_groups(store_g0, 0)
            store_groups(store_g0, 1)
        return norm_acc_finish() if acc else None

    def layer0(mt, hook=None, r0=None, store_g0=None):
        tok0 = mt * T
        r = r0 if r0 is not None else rms_stats(7)
        apply_norm(r, P_AN0, actA, B_actA)
        jobs = []
        for n in range(8):
            for hl in range(2):
                h = (2 * n + hl) % 8
                if n < 4:
                    jobs.append(("qkv%d" % n, hl * 128, actA, B_actA, P_QN0, arena[:, 8 + h, :], [B_ar[8 + h]]))
                else:
                    jobs.append(("qkv%d" % n, hl * 128, actA, B_actA, P_KN0, KT0[:, h, tok0:tok0 + T], [B_KT0[h][mt]]))
        fillers = []
        for n in range(4):
            for st in range(NST):
                def vgrp(n=n, st=st):
                    slot = SS.get("qkv%d" % (8 + n))
                    sv = slab_view(slot, 256)
                    vb = (7, 1)[st % 2]
                    j = 4 * mt + st

                    def f(e):
                        for kc in range(8):
                            i_ = e.matmul(bank(vb, 0, 256), lhsT=actA[:, kc, st * 128:(st + 1) * 128], rhs=sv[:, kc, 0:256], start=(kc == 0), stop=(kc == 7))
                        return i_

                    S.add("pe", f, reads=[B_ring[slot]] + B_actA, writes=[B_bank[vb]])
                    S.add("act", lambda e: e.activation(out=V0[:, j, 2 * n:2 * n + 2, 0:128],
                                                        in_=bank(vb, 0, 256).rearrange("p (h e) -> p h e", h=2), func=AF.Copy),
                          reads=[B_bank[vb]], writes=[B_V0[j][n]])
                fillers.append(vgrp)
        qk_pipeline(jobs, fillers)
        nch = 4 * mt + 4
        pending = []
        pre = None
        for h in range(8):
            def s_op(h, j):
                qTh = arena[:, 8 + h, :]
                st0 = max(0, j - 4 * mt)
                N = (NST - st0) * 128
                sbuf_i = j % 2
                b0 = 2 * sbuf_i
                diag = j >= 4 * mt

                def f(e):
                    i_ = e.matmul(bank(b0, 0, N), lhsT=KT0[0:64, h, j * 128:(j + 1) * 128], rhs=qTh[0:64, st0 * 128:T], start=True, stop=not diag)
                    i_ = e.matmul(bank(b0 + 1, 0, N), lhsT=KT0[64:128, h, j * 128:(j + 1) * 128], rhs=qTh[64:128, st0 * 128:T], start=True, stop=not diag,
                                  tile_position=(64, 0))
                    if diag:
                        i_ = e.matmul(bank(b0, 0, 128), lhsT=cc(C_ID), rhs=cc(C_MCUR), start=False, stop=True)
                        i_ = e.matmul(bank(b0 + 1, 0, 128), lhsT=cc(C_ID), rhs=cc(C_MCUR), start=False, stop=True)
                    return i_

                S.add("pe", f, reads=[B_KT0[h][j // 4], B_ar[8 + h]] + CONST, writes=[B_bank[b0], B_bank[b0 + 1]])
                p = rot("PT", 3)
                S.add("act", lambda e: e.activation(out=PT[p][:].rearrange("p (c n) -> p c n", c=2)[:, :, 0:N],
                                                    in_=ps[:, b0 * 512:(b0 + 2) * 512].rearrange("p (c n) -> p c n", c=2)[:, :, 0:N],
                                                    func=AF.Exp, scale=0.125),
                      reads=[B_bank[b0], B_bank[b0 + 1]], writes=[B_PT[p]])
                return (j, st0, p)

            def av_op(h, j, st0, p):
                def f(e):
                    for st in range(st0, NST):
                        for c in range(2):
                            i_ = e.matmul(bank(4 + st, c * 256, 129), lhsT=PT[p][:, c * 512 + (st - st0) * 128: c * 512 + (st - st0) * 128 + 128],
                                          rhs=V0[:, j, h, 0:129], start=(j == 0 and c == 0), stop=(j == 4 * mt + st), skip_group_check=True)
                    return i_

                S.add("pe", f, reads=[B_PT[p], B_V0[j][h // 2]], writes=[B_bank[4 + st] for st in range(st0, NST)])

            prev = None
            for j in range(nch):
                if j == 0 and pre is not None:
                    cur, pre = pre, None
                else:
                    cur = s_op(h, j)
                if prev is not None:
                    av_op(h, *prev)
                prev = cur
                if j == 2 and pending:
                    pending.pop(0)()
            if h + 1 < 8:
                pre = s_op(h + 1, 0)
            av_op(h, *prev)
            k = rot("ep", 2)
            psv = ps[:, 4 * 512:8 * 512].rearrange("p (s c n) -> p s c n", s=4, c=2)
            S.add("dve", lambda e, k=k: e.reciprocal(out=rs[k][:], in_=psv[:, :, :, 128]), reads=B_bank[4:8], writes=[B_rs[k]])
            S.add("dve", lambda e, k=k: e.tensor_tensor(out=eo[k][:], in0=psv[:, :, 0, 0:128], in1=rs[k][:, :, 0:1].to_broadcast([128, 4, 128]), op=ALU.mult),
                  reads=B_bank[4:8] + [B_rs[k]], writes=[B_eo[k]])
            S.add("dve", lambda e, k=k: e.tensor_tensor(out=et[k][:], in0=psv[:, :, 1, 0:128], in1=rs[k][:, :, 1:2].to_broadcast([128, 4, 128]), op=ALU.mult),
                  reads=B_bank[4:8] + [B_rs[k]], writes=[B_et[k]])
            S.add("dve", lambda e, k=k: e.scalar_tensor_tensor(out=eo[k][:], in0=et[k][:], scalar=neglam[:, 0:1], in1=eo[k][:], op0=ALU.mult, op1=ALU.add),
                  reads=[B_eo[k], B_et[k]] + CONST, writes=[B_eo[k]])
            S.add("dve", lambda e, k=k: e.tensor_tensor(out=et[k][:], in0=eo[k][:], in1=eo[k][:], op=ALU.mult), reads=[B_eo[k]], writes=[B_et[k]])
            S.add("dve", lambda e, k=k: e.tensor_reduce(out=ssq[k][:], in_=et[k][:], axis=AX.X, op=ALU.add), reads=[B_et[k]], writes=[B_ssq[k]])
            obv = arena[:, 0:8, :].rearrange("p (s two) (q e) -> p s two q e", two=2, e=128)[:, :, h // 4, h % 4, :]

            def tail(k=k, obv=obv, h=h):
                S.add("act", lambda e: e.activation(out=ssq[k][:], in_=ssq[k][:], func=AF.Ln, bias=eps128[:, 0:1], scale=1.0),
                      reads=[B_ssq[k]] + CONST, writes=[B_ssq[k]])
                S.add("act", lambda e: e.activation(out=ssq[k][:], in_=ssq[k][:], func=AF.Exp, scale=-0.5), reads=[B_ssq[k]], writes=[B_ssq[k]])
                S.add("dve", lambda e: e.tensor_tensor(out=obv, in0=eo[k][:], in1=ssq[k][:].unsqueeze(2).to_broadcast([128, 4, 128]), op=ALU.mult),
                      reads=[B_eo[k], B_ssq[k]], writes=[B_obf[h]] + [B_ar[2 * st_ + h // 4] for st_ in range(NST)])

            pending.append(tail)
        while pending:
            pending.pop(0)()
        for c in range(8):
            bk = c % 4
            pb = bank(bk).bitcast(BF16)[:, 0:512]

            def tr(e, c=c, pb=pb):
                for st in range(NST):
                    i_ = e.transpose(out=pb[:, st * 128:(st + 1) * 128], in_=arena[:, 2 * st + c // 4, (c % 4) * 128:(c % 4) * 128 + 128], identity=cc(C_ID))
                return i_

            S.add("pe", tr, reads=[B_obf[c]] + CONST, writes=[B_bank[bk]], holds=[B_ar[2 * st_ + c // 4] for st_ in range(NST)])
            if c % 2 == 0:
                S.add("act", lambda e, c=c, pb=pb: e.activation(out=actA[:, c, :], in_=pb, func=AF.Copy, scale=subg[:, 0:1]),
                      reads=[B_bank[bk]] + CONST, writes=[B_actA[c]])
            else:
                S.add("dve", lambda e, c=c, pb=pb: e.tensor_scalar(out=actA[:, c, :], in0=pb, scalar1=subg[:, 0:1], scalar2=None, op0=ALU.mult),
                      reads=[B_bank[bk]] + CONST, writes=[B_actA[c]])
        r = out_proj("wo0", actA, B_actA, split_first=True)
        if hook is not None:
            hook[0]()
        return ffn(0, P_FN0, r, hook[1] if hook is not None else None, acc=(1 in layers), store_g0=store_g0)

    def layer1(mt, hook=None, r=None, store_g0=None):
        tok0 = mt * T
        if r is None:
            r = rms_stats(7)
        apply_norm(r, P_KVN, actA, B_actA)
        pre = apply_norm(r, P_AN1, actB, B_actB, defer=True)
        if not F_PREFILL:
            for o_ in pre:
                o_()
            pre = []
        jobs = []
        for g2 in range(2):
            for gl in range(2):
                g = 2 * g2 + gl
                jobs.append(("kd%d" % g2, gl * 128, actA, B_actA, P_KN1, K1T[:, g, 128:640], [B_K1T[g]]))
        for n in range(4):
            for cl in range(2):
                ci = 2 * n + cl
                jobs.append(("q1_%d" % n, cl * 128, actB, B_actB, P_QN1, arena[:, 8 + ci, :], [B_ar[8 + ci]]))
        fillers = []
        for st in range(NST):
            def vgrp(st=st):
                slot = SS.get("v1")
                sv = slab_view(slot, 256)
                vb = (7, 0)[st % 2]

                def f(e):
                    for kc in range(8):
                        i_ = e.matmul(bank(vb, 0, 256), lhsT=actA[:, kc, st * 128:(st + 1) * 128], rhs=sv[:, kc, 0:256], start=(kc == 0), stop=(kc == 7))
                    return i_

                S.add("pe", f, reads=[B_ring[slot]] + B_actA, writes=[B_bank[vb]])
                S.add("act", lambda e: e.activation(out=V1[:, 1 + st, :, 0:64], in_=bank(vb, 0, 256).rearrange("p (g d) -> p g d", g=4), func=AF.Copy),
                      reads=[B_bank[vb]], writes=[B_V1[1 + st]])
            fillers.append(vgrp)
        qk_pipeline(jobs, fillers, pre)
        units = []
        for st in range(NST):
            iseq = 4 * mt + st
            blks = []
            if iseq > 0:
                blks.append(("prev", st * 128, st, C_M01PREV))
            blks.append(("cur", 128 + st * 128, st + 1, C_M01CUR))
            for g in range(4):
                for bi, blk in enumerate(blks):
                    units.append((st, g, bi, len(blks), blk))

        def s_unit(ui):
            st, g, bi, nb, (bn, kc0, vblk, moff) = units[ui]
            b0 = 2 * (ui % 2)
            kbufs = [B_K1T[g]] + ([B_K1Tprev[g]] if (bn == "prev" and st == 0) else [])

            def f(e):
                e.matmul(bank(b0, 0, 256).rearrange("p (a b) -> p a b", a=2), lhsT=K1T[0:64, g, kc0:kc0 + 128],
                         rhs=arena[0:64, 8 + 2 * g:8 + 2 * g + 2, st * 128:(st + 1) * 128], start=True, stop=True)
                return e.matmul(bank(b0 + 1, 0, 256).rearrange("p (a b) -> p a b", a=2), lhsT=K1T[64:128, g, kc0:kc0 + 128],
                                rhs=arena[64:128, 8 + 2 * g:8 + 2 * g + 2, st * 128:(st + 1) * 128], start=True, stop=True, tile_position=(64, 0))

            S.add("pe", f, reads=kbufs + [B_ar[8 + 2 * g], B_ar[8 + 2 * g + 1]] + CONST, writes=[B_bank[b0], B_bank[b0 + 1]])
            p = rot("PT", 3)
            S.add("act", lambda e: e.activation(out=PT[p][:].rearrange("p (c n) -> p c n", c=2)[:, :, 0:256],
                                                in_=ps[:, b0 * 512:(b0 + 2) * 512].rearrange("p (c n) -> p c n", c=2)[:, :, 0:256],
                                                func=AF.Exp, scale=0.125),
                  reads=[B_bank[b0], B_bank[b0 + 1]], writes=[B_PT[p]])
            for half, eng_ in ((0, "pool"), (1, "dve")):
                pv = PT[p][:, half * 512:half * 512 + 256].rearrange("p (a b) -> p a b", a=2)
                S.add(eng_, lambda e, pv=pv: e.tensor_tensor(out=pv, in0=pv, in1=cc(moff).unsqueeze(1).to_broadcast([128, 2, 128]), op=ALU.mult),
                      reads=[B_PT[p]] + CONST, writes=[B_PT[p]])
            return p

        def av_unit(ui, p):
            st, g, bi, nb, (bn, kc0, vblk, moff) = units[ui]

            def av(e):
                first = True
                for half in range(2):
                    for idx in range(2):
                        hd = 4 * g + 2 * idx + half
                        i_ = e.matmul(bank(4 + hd // 4, (hd % 4) * 68, 65), lhsT=PT[p][:, half * 512 + idx * 128: half * 512 + idx * 128 + 128],
                                      rhs=V1[:, vblk, g, 0:65], start=(bi == 0 and first), stop=(bi == nb - 1), skip_group_check=True)
                        first = False
                return i_

            S.add("pe", av, reads=[B_PT[p], B_V1[vblk]], writes=[B_bank[4 + g]])
            if bi == nb - 1:
                epilogue1(st, g)

        def epilogue1(st, b):
            if True:
                k = rot("den", 2)
                ov = bank(4 + b, 0, 272).rearrange("p (h e) -> p h e", e=68)
                S.add("dve", lambda e, k=k, ov=ov, b=b: e.tensor_tensor(out=den[k][:], in0=ov[:, :, 64], in1=esink[:, 4 * b:4 * b + 4], op=ALU.add),
                      reads=[B_bank[4 + b]] + CONST, writes=[B_den[k]])
                S.add("dve", lambda e, k=k: e.reciprocal(out=den[k][:], in_=den[k][:]), reads=[B_den[k]], writes=[B_den[k]])
                dstv = arena[:, 2 * st + b // 2, (b % 2) * 256:(b % 2) * 256 + 256].rearrange("p (h d) -> p h d", h=4)
                S.add("dve", lambda e, k=k, ov=ov, dstv=dstv: e.tensor_tensor(out=dstv, in0=ov[:, :, 0:64], in1=den[k][:].unsqueeze(2).to_broadcast([128, 4, 64]), op=ALU.mult),
                      reads=[B_bank[4 + b], B_den[k]], writes=[B_ar[2 * st + b // 2]])

        fifo = []
        for ui in range(len(units)):
            fifo.append((ui, s_unit(ui)))
            if len(fifo) > 2:
                av_unit(*fifo.pop(0))
        while fifo:
            av_unit(*fifo.pop(0))
        if mt < 3:
            S.add("pool", lambda e: e.tensor_copy(out=K1T[:, :, 0:128], in_=K1T[:, :, 512:640]), reads=B_K1T, writes=B_K1Tprev)
            S.add("pool", lambda e: e.tensor_copy(out=V1[:, 0, :, 0:64], in_=V1[:, 4, :, 0:64]), reads=[B_V1[4]], writes=[B_V1[0]])
        transposes_to(actA, B_actA, None)
        r = out_proj("wo1", actA, B_actA)
        if hook is not None:
            hook[0]()
        ffn(1, P_FN1, r, hook[1] if hook is not None else None, store_g0=store_g0)

    nmt = S_LEN // T if n_mt is None else n_mt
    for seq in range(nseq):
        for mt in range(nmt):
            g0 = seq * S_LEN + mt * T
            ti_ = seq * nmt + mt
            if ti_ == 0:
                issue_loads(g0)
                trig_tables(seq, mt * T)
            r0 = load_x(g0)
            hook = None
            if ti_ + 1 < nseq * nmt:
                nseq_, nmt_ = divmod(ti_ + 1, nmt)
                hook = ((lambda g1=nseq_ * S_LEN + nmt_ * T: issue_loads(g1)),
                        ((lambda a=nseq_, b=nmt_: trig_dve(a, b * T)), trig_act))
            r1 = r0
            if 0 in layers:
                r1 = layer0(mt, hook if 1 not in layers else None, r0, g0 if 1 not in layers else None)
            if 1 in layers:
                layer1(mt, hook, r1, g0)
            SS.tile += 1
    S.add("sp", None, reads=B_out)
    S.emit()
    es.close()
    return nc, S.stats + (SBUF_LEFT,)


def host_pack(inputs):
    f = np.float32
    prm = np.zeros((128, NPC), f)

    def pc(v):
        return np.ascontiguousarray(np.asarray(v, f).reshape(8, 128).T)

    prm[:, P_AN0:P_AN0 + 8] = pc(inputs["attn_norm"][0])
    prm[:, P_AN1:P_AN1 + 8] = pc(inputs["attn_norm"][1])
    prm[:, P_FN0:P_FN0 + 8] = pc(inputs["ffn_norm"][0])
    prm[:, P_FN1:P_FN1 + 8] = pc(inputs["ffn_norm"][1])
    prm[:, P_KVN:P_KVN + 8] = pc(inputs["kv_norm"])
    prm[:, P_QN0] = np.asarray(inputs["da_q_norm"], f)[0].reshape(128)
    prm[:, P_KN0] = np.asarray(inputs["da_k_norm"], f)[0].reshape(128)
    prm[:, P_SUB] = np.asarray(inputs["da_subln"], f)[0]
    prm[:, P_KN1] = np.tile(np.asarray(inputs["k_norm"], f), 2)
    prm[:, P_QN1] = np.tile(np.asarray(inputs["sw_q_norm"], f)[0], 2)
    prm[:, P_SINK:P_SINK + 16] = np.asarray(inputs["sw_sinks"], f)[0][None, :]
    prm[:, P_LAM:P_LAM + 256] = np.asarray(inputs["da_lambda"], f)[0].reshape(1, 256)
    inv = (1.0 / (np.float32(10000.0) ** (np.arange(0, 64, 2, dtype=f) / np.float32(64)))).astype(f)
    prm[:, P_INVF] = np.tile(inv, 4)
    c = np.zeros((128, NCC), f)
    ar = np.arange(128)
    c[:, C_ID:C_ID + 128] = np.eye(128, dtype=f)
    c[:, C_MCUR:C_MCUR + 128] = np.where(ar[:, None] <= ar[None, :], 0.0, NEG)
    c[:, C_MPREV:C_MPREV + 128] = np.where(ar[:, None] > ar[None, :], 0.0, NEG)
    c[:, C_BONES:C_BONES + 128] = np.where((ar[:, None] // 64) == (ar[None, :] // 64), 1.0 / 64.0, 0.0)
    c[:, C_OMEAN:C_OMEAN + 128] = 1.0 / 1024.0
    R = np.zeros((128, 128), f)
    for m in range(128):
        if (m % 64) < 32:
            R[m + 32, m] = -1.0
        else:
            R[m - 32, m] = 1.0
    c[:, C_RPERM:C_RPERM + 128] = R
    c[:, C_M01CUR:C_M01CUR + 128] = np.where(ar[:, None] <= ar[None, :], 1.0, 0.0)
    c[:, C_M01PREV:C_M01PREV + 128] = np.where(ar[:, None] > ar[None, :], 1.0, 0.0)
    return prm, c


_PROG = {}


def kernel(**inputs):
    n_cores = 8
    nseq = 2
    if "p" not in _PROG:
        _PROG["p"] = build_program(nseq=nseq)[0]
    nc = _PROG["p"]
    prm, c = host_pack(inputs)
    x = np.ascontiguousarray(np.asarray(inputs["x"], np.float32))
    pos = np.ascontiguousarray(np.asarray(inputs["positions"], np.int32))
    shared = {k: np.ascontiguousarray(np.asarray(inputs[k], np.float32)) for k in
              ("da_w_qkv", "da_w_o", "w_gate_up", "w_down", "w_kv", "sw_w_q", "sw_w_o")}
    in_maps = []
    for i in range(n_cores):
        m = dict(shared)
        m["x"] = x[i * nseq:(i + 1) * nseq].reshape(nseq * S_LEN, D)
        m["pos"] = pos[i * nseq:(i + 1) * nseq]
        m["params"] = prm
        m["consts"] = c
        in_maps.append(m)
    res = run_bass_kernel_spmd(nc, in_maps, core_ids=list(range(n_cores)))
    out = np.concatenate([r["out"].reshape(nseq, S_LEN, D) for r in res.results], axis=0)
    return out.astype(np.float32)
```

```python
import math
from contextlib import ExitStack

import numpy as np
import concourse.bass as bass
import concourse.mybir as mybir
from concourse.bass_utils import run_bass_kernel_spmd

F32 = mybir.dt.float32
BF16 = mybir.dt.bfloat16
I32 = mybir.dt.int32
ALU = mybir.AluOpType
AF = mybir.ActivationFunctionType
AX = mybir.AxisListType

D = 1024
S_LEN = 2048
T = 512
NST = 4
HID = 2816
NHC = 22
EPS = 1e-6
NRING = 8
LAMBDA_INIT = 0.8 - 0.6 * math.exp(-0.3 * 0)
NEG = -30000.0
import os as _os
F_STORE_EARLY = _os.environ.get("F_STORE_EARLY", "0") == "1"
F_PREFILL = _os.environ.get("F_PREFILL", "1") == "1"
F_TRIG_LATE = _os.environ.get("F_TRIG_LATE", "1") == "1"

P_AN0, P_AN1, P_FN0, P_FN1, P_KVN = 0, 8, 16, 24, 32
P_QN0, P_KN0, P_SUB, P_KN1, P_QN1 = 40, 41, 42, 43, 44
P_SINK = 45
P_LAM = 61
P_INVF = 61 + 256
NPC = P_INVF + 1
C_ID, C_MCUR, C_MPREV, C_BONES, C_OMEAN, C_RPERM, C_M01CUR, C_M01PREV = 0, 128, 256, 384, 512, 640, 768, 896
NCC = 1024


class Buf:
    __slots__ = ("name", "w", "rs", "rdma")

    def __init__(self, name):
        self.name = name
        self.w = None
        self.rs = {}
        self.rdma = []


class Op:
    __slots__ = ("eng", "fn", "deps", "need_inc", "is_dma", "dmabuf", "sem", "val")


class Sched:
    def __init__(self, nc, es):
        self.nc = nc
        self.es = es
        self.ops = []
        self.engs = {"pe": nc.tensor, "act": nc.scalar, "dve": nc.vector, "pool": nc.gpsimd, "sp": nc.sync}

    def add(self, eng, fn, reads=(), writes=(), dma=False, dmabuf=None, holds=()):
        op = Op()
        op.eng = eng
        op.fn = fn
        op.is_dma = dma
        op.dmabuf = dmabuf
        op.need_inc = False
        op.sem = None
        op.val = 0
        deps = {}

        def dep(d, raw):
            if d is op:
                return
            if (not d.is_dma) and (not dma) and d.eng == eng and eng == "pe":
                return
            deps[id(d)] = d

        for b in reads:
            if b.w is not None:
                dep(b.w, True)
        for b in writes:
            if b.w is not None:
                dep(b.w, False)
            for r in b.rs.values():
                dep(r, False)
            for r in b.rdma:
                dep(r, False)
        for b in list(reads) + list(holds):
            if dma:
                b.rdma.append(op)
            else:
                b.rs[eng] = op
        for b in writes:
            b.w = op
            b.rs = {}
            b.rdma = []
        op.deps = list(deps.values())
        for d in op.deps:
            d.need_inc = True
        self.ops.append(op)
        return op

    def emit(self):
        nc = self.nc
        sems = {}
        cnt = {}
        waited = {e: {} for e in self.engs}

        def getsem(key):
            if key not in sems:
                sems[key] = self.es.enter_context(nc.semaphore("s%d" % len(sems)))
                cnt[key] = 0
            return sems[key]

        for op in self.ops:
            eo = self.engs[op.eng]
            for d in op.deps:
                k = id(d.sem)
                if waited[op.eng].get(k, 0) < d.val:
                    eo.wait_ge(d.sem, d.val)
                    waited[op.eng][k] = d.val
            inst = op.fn(eo) if op.fn is not None else None
            if op.is_dma:
                key = ("dma", id(op.dmabuf), op.eng == "pool")
                s = getsem(key)
                insts = inst if isinstance(inst, (list, tuple)) else [inst]
                for i_ in insts:
                    cnt[key] += 16
                    i_.then_inc(s, 16)
                op.sem, op.val = s, cnt[key]
            elif op.need_inc:
                key = ("eng", op.eng)
                s = getsem(key)
                cnt[key] += 1
                inst.then_inc(s, 1)
                op.sem, op.val = s, cnt[key]
        self.stats = (len(self.ops), len(sems), {k[1]: v for k, v in cnt.items() if k[0] == "eng"})


def slab_table():
    sl = []
    for n in range(12):
        sl.append(("qkv%d" % n, [("da_w_qkv", 0, 0, 8, 256 * n, 256, 0, 256)]))
    for n in range(4):
        sl.append(("wo0_%d" % n, [("da_w_o", 0, 0, 8, 256 * n, 256, 0, 256)]))
    for l in range(2):
        if l == 1:
            for g2 in range(2):
                pcs = []
                for gl in range(2):
                    for dup in range(2):
                        pcs.append(("w_kv", None, 0, 8, (2 * g2 + gl) * 64, 64, gl * 128 + dup * 64, 256))
                sl.append(("kd%d" % g2, pcs))
            sl.append(("v1", [("w_kv", None, 0, 8, 256, 256, 0, 256)]))
            for n in range(4):
                sl.append(("q1_%d" % n, [("sw_w_q", 0, 0, 8, 256 * n, 256, 0, 256)]))
            for n in range(4):
                sl.append(("wo1_%d" % n, [("sw_w_o", 0, 0, 8, 256 * n, 256, 0, 256)]))
        for i in range(NHC):
            sl.append(("gu%d_%d" % (l, i), [("w_gate_up", l, 0, 8, 128 * i, 128, 0, 256),
                                            ("w_gate_up", l, 0, 8, HID + 128 * i, 128, 128, 256)]))
        for o in range(8):
            for hf in range(2):
                sl.append(("dn%d_%d_%d" % (l, o, hf), [("w_down", l, 11 * hf, 11, 128 * o, 128, 0, 128)]))
    return sl


def build_program(nseq=2, layers=(0, 1), n_mt=None, prologue=True):
    nc = bass.Bass("TRN2", target_bir_lowering=False)
    es = ExitStack()
    NTOK = nseq * S_LEN
    dr = {}
    dr["x"] = nc.dram_tensor("x", [NTOK, D], F32, kind="ExternalInput").ap()
    dr["pos"] = nc.dram_tensor("pos", [nseq, S_LEN], I32, kind="ExternalInput").ap()
    dr["da_w_qkv"] = nc.dram_tensor("da_w_qkv", [1, D, 3072], F32, kind="ExternalInput").ap()
    dr["da_w_o"] = nc.dram_tensor("da_w_o", [1, D, D], F32, kind="ExternalInput").ap()
    dr["w_gate_up"] = nc.dram_tensor("w_gate_up", [2, D, 2 * HID], F32, kind="ExternalInput").ap()
    dr["w_down"] = nc.dram_tensor("w_down", [2, HID, D], F32, kind="ExternalInput").ap()
    dr["w_kv"] = nc.dram_tensor("w_kv", [D, 512], F32, kind="ExternalInput").ap()
    dr["sw_w_q"] = nc.dram_tensor("sw_w_q", [1, D, D], F32, kind="ExternalInput").ap()
    dr["sw_w_o"] = nc.dram_tensor("sw_w_o", [1, D, D], F32, kind="ExternalInput").ap()
    dr["params"] = nc.dram_tensor("params", [128, NPC], F32, kind="ExternalInput").ap()
    dr["consts"] = nc.dram_tensor("consts", [128, NCC], F32, kind="ExternalInput").ap()
    out_d = nc.dram_tensor("out", [NTOK, D], F32, kind="ExternalOutput").ap()
    slabs = slab_table()
    NSLAB = len(slabs)
    slab_id = {s[0]: i for i, s in enumerate(slabs)}
    wscr = nc.dram_tensor("wscr", [NSLAB, 128, 2048], BF16).ap()

    S = Sched(nc, es)

    def sb(name, shape, dt):
        return es.enter_context(nc.sbuf_tensor(name, shape, dt))

    KT0 = sb("KT0", [128, 8, S_LEN], BF16)
    V0 = sb("V0", [128, 16, 8, 130], BF16)
    xT = sb("xT", [128, 8, T], F32)
    actA = sb("actA", [128, 8, T], BF16)
    actB = sb("actB", [128, 8, T], BF16)
    arena = sb("arena", [128, NHC, T], BF16)
    sqb = [sb("sqb%d" % i, [128, T], BF16) for i in range(2)]
    rstd = [sb("rstd%d" % i, [128, T], F32) for i in range(2)]
    rstdN = sb("rstdN", [128, T], F32)
    qnb = [sb("qnb%d" % i, [128, T], BF16) for i in range(2)]
    t1 = [sb("t1_%d" % i, [128, T], F32) for i in range(2)]
    t2 = [sb("t2_%d" % i, [128, T], F32) for i in range(2)]
    cosT = sb("cosT", [128, T], F32)
    sinT = sb("sinT", [128, T], F32)
    ti = sb("ti", [128, T], I32)
    PT = [sb("PT%d" % i, [128, 1024], BF16) for i in range(3)]
    rs = [sb("rs%d" % i, [128, 4, 2], F32) for i in range(2)]
    rs2 = [sb("rs2_%d" % i, [128, 4], F32) for i in range(2)]
    et = [sb("et%d" % i, [128, 4, 128], F32) for i in range(2)]
    eo = [sb("eo%d" % i, [128, 4, 128], F32) for i in range(2)]
    ssq = [sb("ssq%d" % i, [128, 4], F32) for i in range(2)]
    sg = [sb("sg%d" % i, [128, T], BF16) for i in range(2)]
    ring = [sb("ring%d" % i, [128, 2048], BF16) for i in range(NRING)]
    xs = sb("xs", [128, D], F32)
    ta = xs[:, 0:512]
    tb = xs[:, 512:1024]
    K1T = sb("K1T", [128, 4, 640], BF16)
    V1 = sb("V1", [128, 5, 4, 66], BF16)
    den = [sb("den%d" % i, [128, 4], F32) for i in range(2)]
    esink = sb("esink", [128, 16], F32)
    cbf = sb("cbf", [128, NCC], BF16)
    identf = sb("identf", [128, 128], F32)
    prm = sb("prm", [128, NPC], F32)
    epsb = sb("epsb", [128, 1], F32)
    eps128 = sb("eps128", [128, 1], F32)
    lamw = sb("lamw", [128, 2, 64], F32)
    lam2 = sb("lam2", [128, 2], F32)
    neglam = sb("neglam", [128, 1], F32)
    subg = sb("subg", [128, 1], F32)
    ps = es.enter_context(nc.psum_tensor("ps", [128, 8 * 512], F32))
    SBUF_LEFT = nc.sbuf_bytes_remaining

    def bank(b, c0=0, n=512):
        return ps[:, b * 512 + c0: b * 512 + c0 + n]

    B_bank = [Buf("bank%d" % i) for i in range(8)]
    B_KT0 = [[Buf("KT0_%d_%d" % (h, m)) for m in range(4)] for h in range(8)]
    B_V0 = [[Buf("V0_%d_%d" % (j, hp)) for hp in range(4)] for j in range(16)]
    B_xT = [Buf("xT%d" % c) for c in range(8)]
    B_actA = [Buf("actA%d" % c) for c in range(8)]
    B_actB = [Buf("actB%d" % c) for c in range(8)]
    B_ar = [Buf("ar%d" % c) for c in range(NHC)]
    B_obf = [Buf("obf%d" % h) for h in range(8)]
    B_sqb = [Buf("sqb%d" % i) for i in range(2)]
    B_rstd = [Buf("rstd%d" % i) for i in range(2)]
    B_rstdN = Buf("rstdN")
    B_qnb = [Buf("qnb%d" % i) for i in range(2)]
    B_t1 = [Buf("t1%d" % i) for i in range(2)]
    B_t2 = [Buf("t2%d" % i) for i in range(2)]
    B_cos, B_sin, B_ti = Buf("cos"), Buf("sin"), Buf("ti")
    B_PT = [Buf("PT%d" % i) for i in range(3)]
    B_rs = [Buf("rs%d" % i) for i in range(2)]
    B_rs2 = [Buf("rs2%d" % i) for i in range(2)]
    B_et = [Buf("et%d" % i) for i in range(2)]
    B_eo = [Buf("eo%d" % i) for i in range(2)]
    B_ssq = [Buf("ssq%d" % i) for i in range(2)]
    B_sg = [Buf("sg%d" % i) for i in range(2)]
    B_ring = [Buf("ring%d" % i) for i in range(NRING)]
    B_xs0, B_xs1 = Buf("xs0"), Buf("xs1")
    B_ta, B_tb = B_xs0, B_xs1
    B_K1T = [Buf("K1T%d" % g) for g in range(4)]
    B_K1Tprev = [Buf("K1Tp%d" % g) for g in range(4)]
    B_V1 = [Buf("V1_%d" % b) for b in range(5)]
    B_den = [Buf("den%d" % i) for i in range(2)]
    B_const = Buf("const")
    B_wscr = [Buf("wscr%d" % i) for i in range(NSLAB)]
    B_out = []

    cnt = {"ring": 0, "sqb": 0, "rstd": 0, "qnb": 0, "t1": 0, "t2": 0, "PT": 0, "ep": 0, "sg": 0, "den": 0}

    def rot(key, n):
        v = cnt[key] % n
        cnt[key] += 1
        return v

    S.add("pool", lambda e: e.dma_start(out=cbf[:], in_=dr["consts"]), writes=[B_const], dma=True, dmabuf=B_const)
    B_c2 = Buf("c2")
    S.add("sp", lambda e: [e.dma_start(out=identf[:], in_=dr["consts"][:, C_ID:C_ID + 128]),
                           e.dma_start(out=prm[:], in_=dr["params"])], writes=[B_c2], dma=True, dmabuf=B_c2)
    B_c3 = Buf("c3")
    S.add("pool", lambda e: e.memset(epsb[:], EPS), writes=[B_c3])
    S.add("pool", lambda e: e.memset(eps128[:], 128.0 * EPS), writes=[B_c3])
    S.add("pool", lambda e: e.memset(V0[:, :, :, 128:130], 1.0), writes=[b for r in B_V0 for b in r])
    S.add("pool", lambda e: e.memset(V1[:, :, :, 64:66], 1.0), writes=B_V1)
    B_l = Buf("lam")
    lamv = prm[:, P_LAM:P_LAM + 256].rearrange("p (a b d) -> p a b d", a=2, b=2)
    S.add("dve", lambda e: e.tensor_tensor(out=lamw[:], in0=lamv[:, :, 0, :], in1=lamv[:, :, 1, :], op=ALU.mult), reads=[B_c2], writes=[B_l])
    S.add("dve", lambda e: e.tensor_reduce(out=lam2[:], in_=lamw[:], axis=AX.X, op=ALU.add), reads=[B_l], writes=[B_l])
    S.add("act", lambda e: e.activation(out=lam2[:], in_=lam2[:], func=AF.Exp), reads=[B_l], writes=[B_l])
    S.add("dve", lambda e: e.tensor_tensor(out=neglam[:], in0=lam2[:, 1:2], in1=lam2[:, 0:1], op=ALU.subtract), reads=[B_l], writes=[B_l])
    S.add("dve", lambda e: e.tensor_scalar(out=neglam[:], in0=neglam[:], scalar1=-LAMBDA_INIT, scalar2=None, op0=ALU.add), reads=[B_l], writes=[B_l])
    S.add("dve", lambda e: e.tensor_scalar(out=subg[:], in0=prm[:, P_SUB:P_SUB + 1], scalar1=(1.0 - LAMBDA_INIT) * math.sqrt(128.0), scalar2=None, op0=ALU.mult), reads=[B_c2], writes=[B_l])
    S.add("act", lambda e: e.activation(out=esink[:], in_=prm[:, P_SINK:P_SINK + 16], func=AF.Exp), reads=[B_c2], writes=[B_l])
    CONST = [B_const, B_c2, B_c3, B_l]

    def cc(off, n=128):
        return cbf[:, off:off + n]

    def src_ap(piece):
        key, l, kc0, nkc, c0, ncol, d0, dstride = piece
        w = dr[key]
        if l is not None:
            w = w[l]
        return w.rearrange("(kc p) n -> p kc n", p=128)[:, kc0:kc0 + nkc, c0:c0 + ncol]

    def dst_ap(slot, piece):
        key, l, kc0, nkc, c0, ncol, d0, dstride = piece
        v = ring[slot][:, 0:nkc * dstride].rearrange("p (kc n) -> p kc n", n=dstride)
        return v[:, :, d0:d0 + ncol]

    LOOK = 6

    class SlabStream:
        def __init__(self, order, n_tiles):
            self.order = order
            self.idx = {nm: i for i, nm in enumerate(order)}
            self.n = len(order)
            self.total = n_tiles * self.n
            self.emitted = 0
            self.slots = {}
            self.tile = 0

        def _emit(self, gi):
            t, k = divmod(gi, self.n)
            name = self.order[k]
            si = slab_id[name]
            slot = rot("ring", NRING)
            if t == 0:
                pieces = slabs[si][1]

                def ld(e):
                    return [e.dma_start(out=dst_ap(slot, p), in_=src_ap(p)) for p in pieces]

                S.add("pool", ld, writes=[B_ring[slot]], dma=True, dmabuf=B_ring[slot])
                S.add("sp", lambda e: e.dma_start(out=wscr[si], in_=ring[slot][:]),
                      reads=[B_ring[slot]], writes=[B_wscr[si]], dma=True, dmabuf=B_ring[slot])
            else:
                S.add("sp", lambda e: e.dma_start(out=ring[slot][:], in_=wscr[si]),
                      reads=[B_wscr[si]], writes=[B_ring[slot]], dma=True, dmabuf=B_ring[slot])
            self.slots[gi] = slot

        def get(self, name):
            gi = self.tile * self.n + self.idx[name]
            upto = min(self.total, gi + 1 + LOOK)
            while self.emitted < upto:
                self._emit(self.emitted)
                self.emitted += 1
            return self.slots[gi]

        fetch = get

    order = []
    if 0 in layers:
        order += ["qkv%d" % n for n in range(12)] + ["wo0_%d" % n for n in range(4)]
        order += ["gu0_%d" % i for i in range(NHC)] + ["dn0_%d_%d" % (o, hf) for o in range(8) for hf in range(2)]
    if 1 in layers:
        order += ["kd0", "kd1"] + ["q1_%d" % n for n in range(4)] + ["v1"] + ["wo1_%d" % n for n in range(4)]
        order += ["gu1_%d" % i for i in range(NHC)] + ["dn1_%d_%d" % (o, hf) for o in range(8) for hf in range(2)]
    SS = SlabStream(order, nseq * (S_LEN // T if n_mt is None else n_mt))

    def slab_view(slot, stride):
        return ring[slot][:].rearrange("p (kc n) -> p kc n", n=stride)

    def rms_stats(ssbank):
        for c in range(8):
            q = rot("sqb", 2)
            S.add("act", lambda e, c=c, q=q: e.activation(out=sqb[q][:], in_=xT[:, c, :], func=AF.Square),
                  reads=[B_xT[c]], writes=[B_sqb[q]])
            S.add("pe", lambda e, c=c, q=q: e.matmul(bank(ssbank), lhsT=cc(C_OMEAN), rhs=sqb[q][:], start=(c == 0), stop=(c == 7)),
                  reads=[B_sqb[q]] + CONST, writes=[B_bank[ssbank]])
        S.add("act", lambda e: e.activation(out=rstdN[:], in_=bank(ssbank), func=AF.Ln, bias=epsb[:, 0:1], scale=1.0),
              reads=[B_bank[ssbank]] + CONST, writes=[B_rstdN])
        S.add("act", lambda e: e.activation(out=rstdN[:], in_=rstdN[:], func=AF.Exp, scale=-0.5),
              reads=[B_rstdN], writes=[B_rstdN])
        return 0

    def apply_norm(r, gcol, dst, B_dst, defer=False):
        ops = []
        for c in range(8):
            def one(c=c):
                S.add("dve", lambda e: e.scalar_tensor_tensor(out=dst[:, c, :], in0=xT[:, c, :], scalar=prm[:, gcol + c:gcol + c + 1],
                                                              in1=rstdN[:], op0=ALU.mult, op1=ALU.mult),
                      reads=[B_xT[c], B_rstdN] + CONST, writes=[B_dst[c]])
            ops.append(one)
        if defer:
            return ops
        for o_ in ops:
            o_()

    def proj_fm(slot, colo, act, B_act, obank, stride=256, split=False):
        sv = slab_view(slot, stride)
        if split:
            for kc in range(8):
                S.add("pe", lambda e, kc=kc: e.matmul(bank(obank), lhsT=sv[:, kc, colo:colo + 128], rhs=act[:, kc, :], start=(kc == 0), stop=(kc == 7)),
                      reads=[B_ring[slot], B_act[kc]], writes=[B_bank[obank]])
            return

        def f(e):
            for kc in range(8):
                i_ = e.matmul(bank(obank), lhsT=sv[:, kc, colo:colo + 128], rhs=act[:, kc, :], start=(kc == 0), stop=(kc == 7))
            return i_

        S.add("pe", f, reads=[B_ring[slot]] + B_act, writes=[B_bank[obank]])

    def qk_pipeline(jobs, fillers=(), prefill=()):
        n = len(jobs)
        RAW, SSB, ROT = (0, 1, 2), (3, 4), (5, 6)
        stt = [dict() for _ in jobs]

        def A(i):
            name, colo, act, B_act_, gcol, dst, B_dst = jobs[i]
            slot = SS.get(name)
            rb = RAW[i % 3]
            proj_fm(slot, colo, act, B_act_, rb, split=(i == 0))
            q = rot("sqb", 2)
            S.add("act", lambda e: e.activation(out=sqb[q][:], in_=bank(rb), func=AF.Square), reads=[B_bank[rb]], writes=[B_sqb[q]])
            stt[i].update(rb=rb, q=q)

        def Bst(i):
            name, colo, act, B_act_, gcol, dst, B_dst = jobs[i]
            rb, q = stt[i]["rb"], stt[i]["q"]
            sbk = SSB[i % 2]
            S.add("pe", lambda e: e.matmul(bank(sbk), lhsT=cc(C_BONES), rhs=sqb[q][:], start=True, stop=True),
                  reads=[B_sqb[q]] + CONST, writes=[B_bank[sbk]])
            r = rot("rstd", 2)
            S.add("act", lambda e: e.activation(out=rstd[r][:], in_=bank(sbk), func=AF.Ln, bias=epsb[:, 0:1], scale=1.0),
                  reads=[B_bank[sbk]] + CONST, writes=[B_rstd[r]])
            S.add("act", lambda e: e.activation(out=rstd[r][:], in_=rstd[r][:], func=AF.Exp, scale=-0.5), reads=[B_rstd[r]], writes=[B_rstd[r]])
            nn = rot("qnb", 2)
            S.add("dve", lambda e: e.scalar_tensor_tensor(out=qnb[nn][:], in0=bank(rb), scalar=prm[:, gcol:gcol + 1], in1=rstd[r][:],
                                                          op0=ALU.mult, op1=ALU.mult),
                  reads=[B_bank[rb], B_rstd[r]] + CONST, writes=[B_qnb[nn]])
            stt[i].update(nn=nn)

        def Cst(i):
            name, colo, act, B_act_, gcol, dst, B_dst = jobs[i]
            nn = stt[i]["nn"]
            rtb = ROT[i % 2]
            S.add("pe", lambda e: e.matmul(bank(rtb), lhsT=cc(C_RPERM), rhs=qnb[nn][:], start=True, stop=True),
                  reads=[B_qnb[nn]] + CONST, writes=[B_bank[rtb]])
            a = rot("t1", 2)
            S.add("pool", lambda e: e.tensor_tensor(out=t1[a][:], in0=qnb[nn][:], in1=cosT[:], op=ALU.mult), reads=[B_qnb[nn], B_cos], writes=[B_t1[a]])
            b = rot("t2", 2)
            S.add("dve", lambda e: e.tensor_tensor(out=t2[b][:], in0=bank(rtb), in1=sinT[:], op=ALU.mult), reads=[B_bank[rtb], B_sin], writes=[B_t2[b]])
            S.add("pool" if i % 2 else "dve", lambda e: e.tensor_tensor(out=dst, in0=t1[a][:], in1=t2[b][:], op=ALU.add), reads=[B_t1[a], B_t2[b]], writes=B_dst)

        fillers = list(fillers)
        per = (len(fillers) + 2) // 3
        prefill = list(prefill)
        for step in range(n + 2):
            for _ in range(2):
                if prefill:
                    prefill.pop(0)()
            if step < n:
                A(step)
            if 0 <= step - 1 < n:
                Bst(step - 1)
            if 0 <= step - 2 < n:
                Cst(step - 2)
            if step >= n - 1:
                for _ in range(per):
                    if fillers:
                        fillers.pop(0)()
        while fillers:
            fillers.pop(0)()

    def residual_add(c, obank):
        S.add("dve", lambda e: e.tensor_tensor(out=xT[:, c, :], in0=xT[:, c, :], in1=bank(obank), op=ALU.add),
              reads=[B_xT[c], B_bank[obank]], writes=[B_xT[c]])

    nacc = {"pend": None}

    def norm_acc_chunk(c):
        q = rot("sqb", 2)
        S.add("act", lambda e: e.activation(out=sqb[q][:], in_=xT[:, c, :], func=AF.Square), reads=[B_xT[c]], writes=[B_sqb[q]])
        norm_acc_flush(False)
        nacc["pend"] = (c, q)

    def norm_acc_flush(last):
        if nacc["pend"] is None:
            return
        c, q = nacc["pend"]
        S.add("pe", lambda e: e.matmul(bank(7), lhsT=cc(C_OMEAN), rhs=sqb[q][:], start=(c == 0), stop=(c == 7)),
              reads=[B_sqb[q]] + CONST, writes=[B_bank[7]])
        nacc["pend"] = None

    def norm_acc_finish():
        norm_acc_flush(True)
        S.add("act", lambda e: e.activation(out=rstdN[:], in_=bank(7), func=AF.Ln, bias=epsb[:, 0:1], scale=1.0),
              reads=[B_bank[7]] + CONST, writes=[B_rstdN])
        S.add("act", lambda e: e.activation(out=rstdN[:], in_=rstdN[:], func=AF.Exp, scale=-0.5), reads=[B_rstdN], writes=[B_rstdN])
        return 0

    def trig_dve(seq, tok0):
        S.add("sp", lambda e: e.dma_start(out=ti[:], in_=dr["pos"][seq:seq + 1, tok0:tok0 + T].partition_broadcast(128)),
              writes=[B_ti], dma=True, dmabuf=B_ti)
        S.add("dve", lambda e: e.tensor_copy(out=ta[:], in_=ti[:]), reads=[B_ti], writes=[B_ta])
        S.add("dve", lambda e: e.tensor_scalar(out=ta[:], in0=ta[:], scalar1=prm[:, P_INVF:P_INVF + 1], scalar2=float(1.0 / (2 * np.pi)),
                                               op0=ALU.mult, op1=ALU.mult), reads=[B_ta] + CONST, writes=[B_ta])
        for which, shift, dstt, B_d in (("s", 0.0, sinT, B_sin), ("c", 0.25, cosT, B_cos)):
            S.add("dve", lambda e, shift=shift: e.tensor_scalar(out=tb[:], in0=ta[:], scalar1=shift, scalar2=None, op0=ALU.add),
                  reads=[B_ta], writes=[B_tb])
            S.add("dve", lambda e: e.tensor_copy(out=ti[:], in_=tb[:]), reads=[B_tb], writes=[B_ti])
            S.add("dve", lambda e, dstt=dstt: e.tensor_copy(out=dstt[:], in_=ti[:]), reads=[B_ti], writes=[B_d])
            S.add("dve", lambda e, dstt=dstt: e.tensor_tensor(out=dstt[:], in0=tb[:], in1=dstt[:], op=ALU.subtract), reads=[B_tb, B_d], writes=[B_d])
            S.add("dve", lambda e, dstt=dstt: e.tensor_single_scalar(out=tb[:], in_=dstt[:], scalar=0.5, op=ALU.is_gt), reads=[B_d], writes=[B_tb])
            S.add("dve", lambda e, dstt=dstt: e.tensor_tensor(out=dstt[:], in0=dstt[:], in1=tb[:], op=ALU.subtract), reads=[B_tb, B_d], writes=[B_d])
            S.add("dve", lambda e, dstt=dstt: e.tensor_single_scalar(out=tb[:], in_=dstt[:], scalar=-0.5, op=ALU.is_lt), reads=[B_d], writes=[B_tb])
            S.add("dve", lambda e, dstt=dstt: e.tensor_tensor(out=dstt[:], in0=dstt[:], in1=tb[:], op=ALU.add), reads=[B_tb, B_d], writes=[B_d])

    def trig_act():
        for dstt, B_d in ((sinT, B_sin), (cosT, B_cos)):
            S.add("act", lambda e, dstt=dstt: e.activation(out=dstt[:], in_=dstt[:], func=AF.Sin, scale=float(2 * np.pi)), reads=[B_d], writes=[B_d])

    def trig_tables(seq, tok0):
        trig_dve(seq, tok0)
        trig_act()

    def stage_halves(st):
        if st < 2:
            v = actA[:, 4 * st:4 * st + 4, :].rearrange("p a b -> p (a b)").bitcast(F32)
            return [(v[:, 0:512], B_actA[4 * st:4 * st + 2]), (v[:, 512:1024], B_actA[4 * st + 2:4 * st + 4])]
        tt, BB = (t1, B_t1) if st == 2 else (t2, B_t2)
        return [(tt[0][:], [BB[0]]), (tt[1][:], [BB[1]])]

    def issue_loads(g0):
        for st in range(NST):
            r0 = g0 + st * 128
            for half, (ap_, bufs) in enumerate(stage_halves(st)):
                S.add("sp", lambda e, r0=r0, half=half, ap_=ap_: e.dma_start(out=ap_, in_=dr["x"][r0:r0 + 128, half * 512:(half + 1) * 512]),
                      writes=bufs, dma=True, dmabuf=bufs[0])

    def load_x(g0):
        for half in range(2):
            for st in range(NST):
                ap_, bufs = stage_halves(st)[half]
                bk = (2 * st + half) % 4

                def tr(e, ap_=ap_, bk=bk):
                    for c4 in range(4):
                        i_ = e.transpose(out=bank(bk, c4 * 128, 128), in_=ap_[:, c4 * 128:(c4 + 1) * 128], identity=identf[:])
                    return i_

                S.add("pe", tr, reads=bufs + CONST, writes=[B_bank[bk]])
                if half == 0:
                    S.add("act", lambda e, half=half, bk=bk, st=st: e.activation(
                        out=xT[:, half * 4:half * 4 + 4, st * 128:(st + 1) * 128],
                        in_=bank(bk).rearrange("p (c t) -> p c t", c=4), func=AF.Copy),
                        reads=[B_bank[bk]], writes=B_xT[half * 4:half * 4 + 4])
                else:
                    S.add("dve", lambda e, half=half, bk=bk, st=st: e.tensor_copy(
                        out=xT[:, half * 4:half * 4 + 4, st * 128:(st + 1) * 128],
                        in_=bank(bk).rearrange("p (c t) -> p c t", c=4)),
                        reads=[B_bank[bk]], writes=B_xT[half * 4:half * 4 + 4])
            for c in range(half * 4, half * 4 + 4):
                norm_acc_chunk(c)
        return norm_acc_finish()

    stg = [(xs[:, 0:512], B_xs0), (xs[:, 512:1024], B_xs1), (PT[0][:].bitcast(F32), B_PT[0]), (PT[1][:].bitcast(F32), B_PT[1])]
    stq = {"q": 0}

    def store_groups(g0, half):
        for st in range(NST):
            r0 = g0 + st * 128
            bk = (0, 1, 4, 5)[stq["q"] % 4]
            o_ap, o_b = stg[stq["q"] % 4]
            stq["q"] += 1

            def tr(e, half=half, bk=bk, st=st):
                for c4 in range(4):
                    c = half * 4 + c4
                    i_ = e.transpose(out=bank(bk, c4 * 128, 128), in_=xT[:, c, st * 128:(st + 1) * 128], identity=identf[:])
                return i_

            S.add("pe", tr, reads=B_xT[half * 4:half * 4 + 4] + CONST, writes=[B_bank[bk]])
            S.add("dve", lambda e, bk=bk, o_ap=o_ap: e.tensor_copy(out=o_ap, in_=bank(bk)), reads=[B_bank[bk]], writes=[o_b])
            bo = Buf("out")
            B_out.append(bo)
            S.add("sp", lambda e, r0=r0, half=half, o_ap=o_ap: e.dma_start(out=out_d[r0:r0 + 128, half * 512:(half + 1) * 512], in_=o_ap),
                  reads=[o_b], writes=[bo], dma=True, dmabuf=o_b)

    def transposes_to(act, B_act, scale_ap):
        for st in range(NST):
            bk = st % 2
            pb = bank(bk).bitcast(BF16)

            def tr(e, st=st, pb=pb):
                for c in range(8):
                    i_ = e.transpose(out=pb[:, c * 128:(c + 1) * 128], in_=arena[:, 2 * st + c // 4, (c % 4) * 128:(c % 4) * 128 + 128],
                                     identity=cc(C_ID))
                return i_

            S.add("pe", tr, reads=[B_ar[2 * st], B_ar[2 * st + 1]] + CONST, writes=[B_bank[bk]])
            if st % 2 == 1:
                if scale_ap is not None:
                    S.add("dve", lambda e, st=st, pb=pb: e.tensor_scalar(out=act[:, :, st * 128:(st + 1) * 128], in0=pb.rearrange("p (c t) -> p c t", c=8),
                                                                        scalar1=scale_ap, scalar2=None, op0=ALU.mult), reads=[B_bank[bk]] + CONST, writes=B_act)
                else:
                    S.add("dve", lambda e, st=st, pb=pb: e.tensor_copy(out=act[:, :, st * 128:(st + 1) * 128], in_=pb.rearrange("p (c t) -> p c t", c=8)),
                          reads=[B_bank[bk]], writes=B_act)
            elif scale_ap is not None:
                S.add("act", lambda e, st=st, pb=pb: e.activation(out=act[:, :, st * 128:(st + 1) * 128], in_=pb.rearrange("p (c t) -> p c t", c=8),
                                                                 func=AF.Copy, scale=scale_ap), reads=[B_bank[bk]] + CONST, writes=B_act)
            else:
                S.add("act", lambda e, st=st, pb=pb: e.activation(out=act[:, :, st * 128:(st + 1) * 128], in_=pb.rearrange("p (c t) -> p c t", c=8),
                                                                 func=AF.Copy), reads=[B_bank[bk]], writes=B_act)

    def out_proj(prefix, act, B_act, split_first=False):
        for n in range(4):
            slot = SS.fetch("%s_%d" % (prefix, n))
            for cl in range(2):
                o = 2 * n + cl
                ob = 2 + (o % 2)
                proj_fm(slot, cl * 128, act, B_act, ob, split=(split_first and o == 0))
                residual_add(o, ob)
                norm_acc_chunk(o)
        return norm_acc_finish()

    def ffn(l, gcol, r, hook2=None, acc=False, store_g0=None):
        apply_norm(r, gcol, actB, B_actB)
        if hook2 is not None:
            hook2[0]()
        for i in range(NHC):
            slot = SS.fetch("gu%d_%d" % (l, i))
            gb = 4 + (i % 2)
            ub = 6 + (i % 2)
            proj_fm(slot, 0, actB, B_actB, gb, split=(i == 0))
            proj_fm(slot, 128, actB, B_actB, ub)
            s_ = rot("sg", 2)
            S.add("act", lambda e, gb=gb, s_=s_: e.activation(out=sg[s_][:], in_=bank(gb), func=AF.Silu), reads=[B_bank[gb]], writes=[B_sg[s_]])
            S.add("dve", lambda e, ub=ub, s_=s_, i=i: e.tensor_tensor(out=arena[:, i, :], in0=sg[s_][:], in1=bank(ub), op=ALU.mult),
                  reads=[B_sg[s_], B_bank[ub]], writes=[B_ar[i]])
            if i == (8 if F_TRIG_LATE else 0) and hook2 is not None:
                hook2[1]()
        for o in range(8):
            slots = [SS.fetch("dn%d_%d_%d" % (l, o, hf)) for hf in range(2)]
            ob = 2 + (o % 2)

            def f(e, slots=slots, ob=ob):
                for kc in range(NHC):
                    sv = slab_view(slots[kc // 11], 128)
                    i_ = e.matmul(bank(ob), lhsT=sv[:, kc % 11, 0:128], rhs=arena[:, kc, :], start=(kc == 0), stop=(kc == NHC - 1))
                return i_

            S.add("pe", f, reads=[B_ring[slots[0]], B_ring[slots[1]]] + B_ar, writes=[B_bank[ob]])
            residual_add(o, ob)
            if acc:
                norm_acc_chunk(o)
            if store_g0 is not None and o == 4 and F_STORE_EARLY:
                store_groups(store_g0, 0)
        if store_g0 is not None:
            if not F_STORE_EARLY:
                store_groups(store_g0, 0)
            store_groups(store_g0, 1)
        return norm_acc_finish() if acc else None

    def layer0(mt, hook=None, r0=None, store_g0=None):
        tok0 = mt * T
        r = r0 if r0 is not None else rms_stats(7)
        apply_norm(r, P_AN0, actA, B_actA)
        jobs = []
        for n in range(8):
            for hl in range(2):
                h = (2 * n + hl) % 8
                if n < 4:
                    jobs.append(("qkv%d" % n, hl * 128, actA, B_actA, P_QN0, arena[:, 8 + h, :], [B_ar[8 + h]]))
                else:
                    jobs.append(("qkv%d" % n, hl * 128, actA, B_actA, P_KN0, KT0[:, h, tok0:tok0 + T], [B_KT0[h][mt]]))
        fillers = []
        for n in range(4):
            for st in range(NST):
                def vgrp(n=n, st=st):
                    slot = SS.get("qkv%d" % (8 + n))
                    sv = slab_view(slot, 256)
                    vb = (7, 1)[st % 2]
                    j = 4 * mt + st

                    def f(e):
                        for kc in range(8):
                            i_ = e.matmul(bank(vb, 0, 256), lhsT=actA[:, kc, st * 128:(st + 1) * 128], rhs=sv[:, kc, 0:256], start=(kc == 0), stop=(kc == 7))
                        return i_

                    S.add("pe", f, reads=[B_ring[slot]] + B_actA, writes=[B_bank[vb]])
                    S.add("act", lambda e: e.activation(out=V0[:, j, 2 * n:2 * n + 2, 0:128],
                                                        in_=bank(vb, 0, 256).rearrange("p (h e) -> p h e", h=2), func=AF.Copy),
                          reads=[B_bank[vb]], writes=[B_V0[j][n]])
                fillers.append(vgrp)
        qk_pipeline(jobs, fillers)
        nch = 4 * mt + 4
        pending = []
        pre = None
        for h in range(8):
            def s_op(h, j):
                qTh = arena[:, 8 + h, :]
                st0 = max(0, j - 4 * mt)
                N = (NST - st0) * 128
                sbuf_i = j % 2
                b0 = 2 * sbuf_i
                diag = j >= 4 * mt

                def f(e):
                    i_ = e.matmul(bank(b0, 0, N), lhsT=KT0[0:64, h, j * 128:(j + 1) * 128], rhs=qTh[0:64, st0 * 128:T], start=True, stop=not diag)
                    i_ = e.matmul(bank(b0 + 1, 0, N), lhsT=KT0[64:128, h, j * 128:(j + 1) * 128], rhs=qTh[64:128, st0 * 128:T], start=True, stop=not diag,
                                  tile_position=(64, 0))
                    if diag:
                        i_ = e.matmul(bank(b0, 0, 128), lhsT=cc(C_ID), rhs=cc(C_MCUR), start=False, stop=True)
                        i_ = e.matmul(bank(b0 + 1, 0, 128), lhsT=cc(C_ID), rhs=cc(C_MCUR), start=False, stop=True)
                    return i_

                S.add("pe", f, reads=[B_KT0[h][j // 4], B_ar[8 + h]] + CONST, writes=[B_bank[b0], B_bank[b0 + 1]])
                p = rot("PT", 3)
                S.add("act", lambda e: e.activation(out=PT[p][:].rearrange("p (c n) -> p c n", c=2)[:, :, 0:N],
                                                    in_=ps[:, b0 * 512:(b0 + 2) * 512].rearrange("p (c n) -> p c n", c=2)[:, :, 0:N],
                                                    func=AF.Exp, scale=0.125),
                      reads=[B_bank[b0], B_bank[b0 + 1]], writes=[B_PT[p]])
                return (j, st0, p)

            def av_op(h, j, st0, p):
                def f(e):
                    for st in range(st0, NST):
                        for c in range(2):
                            i_ = e.matmul(bank(4 + st, c * 256, 129), lhsT=PT[p][:, c * 512 + (st - st0) * 128: c * 512 + (st - st0) * 128 + 128],
                                          rhs=V0[:, j, h, 0:129], start=(j == 0 and c == 0), stop=(j == 4 * mt + st), skip_group_check=True)
                    return i_

                S.add("pe", f, reads=[B_PT[p], B_V0[j][h // 2]], writes=[B_bank[4 + st] for st in range(st0, NST)])

            prev = None
            for j in range(nch):
                if j == 0 and pre is not None:
                    cur, pre = pre, None
                else:
                    cur = s_op(h, j)
                if prev is not None:
                    av_op(h, *prev)
                prev = cur
                if j == 2 and pending:
                    pending.pop(0)()
            if h + 1 < 8:
                pre = s_op(h + 1, 0)
            av_op(h, *prev)
            k = rot("ep", 2)
            psv = ps[:, 4 * 512:8 * 512].rearrange("p (s c n) -> p s c n", s=4, c=2)
            S.add("dve", lambda e, k=k: e.reciprocal(out=rs[k][:], in_=psv[:, :, :, 128]), reads=B_bank[4:8], writes=[B_rs[k]])
            S.add("dve", lambda e, k=k: e.tensor_tensor(out=eo[k][:], in0=psv[:, :, 0, 0:128], in1=rs[k][:, :, 0:1].to_broadcast([128, 4, 128]), op=ALU.mult),
                  reads=B_bank[4:8] + [B_rs[k]], writes=[B_eo[k]])
            S.add("dve", lambda e, k=k: e.tensor_tensor(out=et[k][:], in0=psv[:, :, 1, 0:128], in1=rs[k][:, :, 1:2].to_broadcast([128, 4, 128]), op=ALU.mult),
                  reads=B_bank[4:8] + [B_rs[k]], writes=[B_et[k]])
            S.add("dve", lambda e, k=k: e.scalar_tensor_tensor(out=eo[k][:], in0=et[k][:], scalar=neglam[:, 0:1], in1=eo[k][:], op0=ALU.mult, op1=ALU.add),
                  reads=[B_eo[k], B_et[k]] + CONST, writes=[B_eo[k]])
            S.add("dve", lambda e, k=k: e.tensor_tensor(out=et[k][:], in0=eo[k][:], in1=eo[k][:], op=ALU.mult), reads=[B_eo[k]], writes=[B_et[k]])
            S.add("dve", lambda e, k=k: e.tensor_reduce(out=ssq[k][:], in_=et[k][:], axis=AX.X, op=ALU.add), reads=[B_et[k]], writes=[B_ssq[k]])
            obv = arena[:, 0:8, :].rearrange("p (s two) (q e) -> p s two q e", two=2, e=128)[:, :, h // 4, h % 4, :]

            def tail(k=k, obv=obv, h=h):
                S.add("act", lambda e: e.activation(out=ssq[k][:], in_=ssq[k][:], func=AF.Ln, bias=eps128[:, 0:1], scale=1.0),
                      reads=[B_ssq[k]] + CONST, writes=[B_ssq[k]])
                S.add("act", lambda e: e.activation(out=ssq[k][:], in_=ssq[k][:], func=AF.Exp, scale=-0.5), reads=[B_ssq[k]], writes=[B_ssq[k]])
                S.add("dve", lambda e: e.tensor_tensor(out=obv, in0=eo[k][:], in1=ssq[k][:].unsqueeze(2).to_broadcast([128, 4, 128]), op=ALU.mult),
                      reads=[B_eo[k], B_ssq[k]], writes=[B_obf[h]] + [B_ar[2 * st_ + h // 4] for st_ in range(NST)])

            pending.append(tail)
        while pending:
            pending.pop(0)()
        for c in range(8):
            bk = c % 4
            pb = bank(bk).bitcast(BF16)[:, 0:512]

            def tr(e, c=c, pb=pb):
                for st in range(NST):
                    i_ = e.transpose(out=pb[:, st * 128:(st + 1) * 128], in_=arena[:, 2 * st + c // 4, (c % 4) * 128:(c % 4) * 128 + 128], identity=cc(C_ID))
                return i_

            S.add("pe", tr, reads=[B_obf[c]] + CONST, writes=[B_bank[bk]], holds=[B_ar[2 * st_ + c // 4] for st_ in range(NST)])
            if c % 2 == 0:
                S.add("act", lambda e, c=c, pb=pb: e.activation(out=actA[:, c, :], in_=pb, func=AF.Copy, scale=subg[:, 0:1]),
                      reads=[B_bank[bk]] + CONST, writes=[B_actA[c]])
            else:
                S.add("dve", lambda e, c=c, pb=pb: e.tensor_scalar(out=actA[:, c, :], in0=pb, scalar1=subg[:, 0:1], scalar2=None, op0=ALU.mult),
                      reads=[B_bank[bk]] + CONST, writes=[B_actA[c]])
        r = out_proj("wo0", actA, B_actA, split_first=True)
        if hook is not None:
            hook[0]()
        return ffn(0, P_FN0, r, hook[1] if hook is not None else None, acc=(1 in layers), store_g0=store_g0)

    def layer1(mt, hook=None, r=None, store_g0=None):
        tok0 = mt * T
        if r is None:
            r = rms_stats(7)
        apply_norm(r, P_KVN, actA, B_actA)
        pre = apply_norm(r, P_AN1, actB, B_actB, defer=True)
        if not F_PREFILL:
            for o_ in pre:
                o_()
            pre = []
        jobs = []
        for g2 in range(2):
            for gl in range(2):
                g = 2 * g2 + gl
                jobs.append(("kd%d" % g2, gl * 128, actA, B_actA, P_KN1, K1T[:, g, 128:640], [B_K1T[g]]))
        for n in range(4):
            for cl in range(2):
                ci = 2 * n + cl
                jobs.append(("q1_%d" % n, cl * 128, actB, B_actB, P_QN1, arena[:, 8 + ci, :], [B_ar[8 + ci]]))
        fillers = []
        for st in range(NST):
            def vgrp(st=st):
                slot = SS.get("v1")
                sv = slab_view(slot, 256)
                vb = (7, 0)[st % 2]

                def f(e):
                    for kc in range(8):
                        i_ = e.matmul(bank(vb, 0, 256), lhsT=actA[:, kc, st * 128:(st + 1) * 128], rhs=sv[:, kc, 0:256], start=(kc == 0), stop=(kc == 7))
                    return i_

                S.add("pe", f, reads=[B_ring[slot]] + B_actA, writes=[B_bank[vb]])
                S.add("act", lambda e: e.activation(out=V1[:, 1 + st, :, 0:64], in_=bank(vb, 0, 256).rearrange("p (g d) -> p g d", g=4), func=AF.Copy),
                      reads=[B_bank[vb]], writes=[B_V1[1 + st]])
            fillers.append(vgrp)
        qk_pipeline(jobs, fillers, pre)
        units = []
        for st in range(NST):
            iseq = 4 * mt + st
            blks = []
            if iseq > 0:
                blks.append(("prev", st * 128, st, C_M01PREV))
            blks.append(("cur", 128 + st * 128, st + 1, C_M01CUR))
            for g in range(4):
                for bi, blk in enumerate(blks):
                    units.append((st, g, bi, len(blks), blk))

        def s_unit(ui):
            st, g, bi, nb, (bn, kc0, vblk, moff) = units[ui]
            b0 = 2 * (ui % 2)
            kbufs = [B_K1T[g]] + ([B_K1Tprev[g]] if (bn == "prev" and st == 0) else [])

            def f(e):
                e.matmul(bank(b0, 0, 256).rearrange("p (a b) -> p a b", a=2), lhsT=K1T[0:64, g, kc0:kc0 + 128],
                         rhs=arena[0:64, 8 + 2 * g:8 + 2 * g + 2, st * 128:(st + 1) * 128], start=True, stop=True)
                return e.matmul(bank(b0 + 1, 0, 256).rearrange("p (a b) -> p a b", a=2), lhsT=K1T[64:128, g, kc0:kc0 + 128],
                                rhs=arena[64:128, 8 + 2 * g:8 + 2 * g + 2, st * 128:(st + 1) * 128], start=True, stop=True, tile_position=(64, 0))

            S.add("pe", f, reads=kbufs + [B_ar[8 + 2 * g], B_ar[8 + 2 * g + 1]] + CONST, writes=[B_bank[b0], B_bank[b0 + 1]])
            p = rot("PT", 3)
            S.add("act", lambda e: e.activation(out=PT[p][:].rearrange("p (c n) -> p c n", c=2)[:, :, 0:256],
                                                in_=ps[:, b0 * 512:(b0 + 2) * 512].rearrange("p (c n) -> p c n", c=2)[:, :, 0:256],
                                                func=AF.Exp, scale=0.125),
                  reads=[B_bank[b0], B_bank[b0 + 1]], writes=[B_PT[p]])
            for half, eng_ in ((0, "dve"), (1, "dve")):
                pv = PT[p][:, half * 512:half * 512 + 256].rearrange("p (a b) -> p a b", a=2)
                S.add(eng_, lambda e, pv=pv: e.tensor_tensor(out=pv, in0=pv, in1=cc(moff).unsqueeze(1).to_broadcast([128, 2, 128]), op=ALU.mult),
                      reads=[B_PT[p]] + CONST, writes=[B_PT[p]])
            return p

        def av_unit(ui, p):
            st, g, bi, nb, (bn, kc0, vblk, moff) = units[ui]

            def av(e):
                first = True
                for half in range(2):
                    for idx in range(2):
                        hd = 4 * g + 2 * idx + half
                        i_ = e.matmul(bank(4 + hd // 4, (hd % 4) * 68, 65), lhsT=PT[p][:, half * 512 + idx * 128: half * 512 + idx * 128 + 128],
                                      rhs=V1[:, vblk, g, 0:65], start=(bi == 0 and first), stop=(bi == nb - 1), skip_group_check=True)
                        first = False
                return i_

            S.add("pe", av, reads=[B_PT[p], B_V1[vblk]], writes=[B_bank[4 + g]])
            if bi == nb - 1:
                epilogue1(st, g)

        def epilogue1(st, b):
            if True:
                k = rot("den", 2)
                ov = bank(4 + b, 0, 272).rearrange("p (h e) -> p h e", e=68)
                S.add("dve", lambda e, k=k, ov=ov, b=b: e.tensor_tensor(out=den[k][:], in0=ov[:, :, 64], in1=esink[:, 4 * b:4 * b + 4], op=ALU.add),
                      reads=[B_bank[4 + b]] + CONST, writes=[B_den[k]])
                S.add("dve", lambda e, k=k: e.reciprocal(out=den[k][:], in_=den[k][:]), reads=[B_den[k]], writes=[B_den[k]])
                dstv = arena[:, 2 * st + b // 2, (b % 2) * 256:(b % 2) * 256 + 256].rearrange("p (h d) -> p h d", h=4)
                S.add("dve", lambda e, k=k, ov=ov, dstv=dstv: e.tensor_tensor(out=dstv, in0=ov[:, :, 0:64], in1=den[k][:].unsqueeze(2).to_broadcast([128, 4, 64]), op=ALU.mult),
                      reads=[B_bank[4 + b], B_den[k]], writes=[B_ar[2 * st + b // 2]])

        fifo = []
        for ui in range(len(units)):
            fifo.append((ui, s_unit(ui)))
            if len(fifo) > 2:
                av_unit(*fifo.pop(0))
        while fifo:
            av_unit(*fifo.pop(0))
        if mt < 3:
            S.add("pool", lambda e: e.tensor_copy(out=K1T[:, :, 0:128], in_=K1T[:, :, 512:640]), reads=B_K1T, writes=B_K1Tprev)
            S.add("pool", lambda e: e.tensor_copy(out=V1[:, 0, :, 0:64], in_=V1[:, 4, :, 0:64]), reads=[B_V1[4]], writes=[B_V1[0]])
        transposes_to(actA, B_actA, None)
        r = out_proj("wo1", actA, B_actA)
        if hook is not None:
            hook[0]()
        ffn(1, P_FN1, r, hook[1] if hook is not None else None, store_g0=store_g0)

    nmt = S_LEN // T if n_mt is None else n_mt
    for seq in range(nseq):
        for mt in range(nmt):
            g0 = seq * S_LEN + mt * T
            ti_ = seq * nmt + mt
            if ti_ == 0:
                issue_loads(g0)
                trig_tables(seq, mt * T)
            r0 = load_x(g0)
            hook = None
            if ti_ + 1 < nseq * nmt:
                nseq_, nmt_ = divmod(ti_ + 1, nmt)
                hook = ((lambda g1=nseq_ * S_LEN + nmt_ * T: issue_loads(g1)),
                        ((lambda a=nseq_, b=nmt_: trig_dve(a, b * T)), trig_act))
            r1 = r0
            if 0 in layers:
                r1 = layer0(mt, hook if 1 not in layers else None, r0, g0 if 1 not in layers else None)
            if 1 in layers:
                layer1(mt, hook, r1, g0)
            SS.tile += 1
    S.add("sp", None, reads=B_out)
    S.emit()
    es.close()
    return nc, S.stats + (SBUF_LEFT,)


def host_pack(inputs):
    f = np.float32
    prm = np.zeros((128, NPC), f)

    def pc(v):
        return np.ascontiguousarray(np.asarray(v, f).reshape(8, 128).T)

    prm[:, P_AN0:P_AN0 + 8] = pc(inputs["attn_norm"][0])
    prm[:, P_AN1:P_AN1 + 8] = pc(inputs["attn_norm"][1])
    prm[:, P_FN0:P_FN0 + 8] = pc(inputs["ffn_norm"][0])
    prm[:, P_FN1:P_FN1 + 8] = pc(inputs["ffn_norm"][1])
    prm[:, P_KVN:P_KVN + 8] = pc(inputs["kv_norm"])
    prm[:, P_QN0] = np.asarray(inputs["da_q_norm"], f)[0].reshape(128)
    prm[:, P_KN0] = np.asarray(inputs["da_k_norm"], f)[0].reshape(128)
    prm[:, P_SUB] = np.asarray(inputs["da_subln"], f)[0]
    prm[:, P_KN1] = np.tile(np.asarray(inputs["k_norm"], f), 2)
    prm[:, P_QN1] = np.tile(np.asarray(inputs["sw_q_norm"], f)[0], 2)
    prm[:, P_SINK:P_SINK + 16] = np.asarray(inputs["sw_sinks"], f)[0][None, :]
    prm[:, P_LAM:P_LAM + 256] = np.asarray(inputs["da_lambda"], f)[0].reshape(1, 256)
    inv = (1.0 / (np.float32(10000.0) ** (np.arange(0, 64, 2, dtype=f) / np.float32(64)))).astype(f)
    prm[:, P_INVF] = np.tile(inv, 4)
    c = np.zeros((128, NCC), f)
    ar = np.arange(128)
    c[:, C_ID:C_ID + 128] = np.eye(128, dtype=f)
    c[:, C_MCUR:C_MCUR + 128] = np.where(ar[:, None] <= ar[None, :], 0.0, NEG)
    c[:, C_MPREV:C_MPREV + 128] = np.where(ar[:, None] > ar[None, :], 0.0, NEG)
    c[:, C_BONES:C_BONES + 128] = np.where((ar[:, None] // 64) == (ar[None, :] // 64), 1.0 / 64.0, 0.0)
    c[:, C_OMEAN:C_OMEAN + 128] = 1.0 / 1024.0
    R = np.zeros((128, 128), f)
    for m in range(128):
        if (m % 64) < 32:
            R[m + 32, m] = -1.0
        else:
            R[m - 32, m] = 1.0
    c[:, C_RPERM:C_RPERM + 128] = R
    c[:, C_M01CUR:C_M01CUR + 128] = np.where(ar[:, None] <= ar[None, :], 1.0, 0.0)
    c[:, C_M01PREV:C_M01PREV + 128] = np.where(ar[:, None] > ar[None, :], 1.0, 0.0)
    return prm, c


_PROG = {}


def kernel(**inputs):
    n_cores = 8
    nseq = 2
    if "p" not in _PROG:
        _PROG["p"] = build_program(nseq=nseq)[0]
    nc = _PROG["p"]
    prm, c = host_pack(inputs)
    x = np.ascontiguousarray(np.asarray(inputs["x"], np.float32))
    pos = np.ascontiguousarray(np.asarray(inputs["positions"], np.int32))
    shared = {k: np.ascontiguousarray(np.asarray(inputs[k], np.float32)) for k in
              ("da_w_qkv", "da_w_o", "w_gate_up", "w_down", "w_kv", "sw_w_q", "sw_w_o")}
    in_maps = []
    for i in range(n_cores):
        m = dict(shared)
        m["x"] = x[i * nseq:(i + 1) * nseq].reshape(nseq * S_LEN, D)
        m["pos"] = pos[i * nseq:(i + 1) * nseq]
        m["params"] = prm
        m["consts"] = c
        in_maps.append(m)
    res = run_bass_kernel_spmd(nc, in_maps, core_ids=list(range(n_cores)))
    out = np.concatenate([r["out"].reshape(nseq, S_LEN, D) for r in res.results], axis=0)
    return out.astype(np.float32)
```

```python
import math
from contextlib import ExitStack

import numpy as np
import concourse.bass as bass
import concourse.mybir as mybir
from concourse.bass_utils import run_bass_kernel_spmd

F32 = mybir.dt.float32
BF16 = mybir.dt.bfloat16
I32 = mybir.dt.int32
ALU = mybir.AluOpType
AF = mybir.ActivationFunctionType
AX = mybir.AxisListType

D = 1024
S_LEN = 2048
T = 512
NST = 4
HID = 2816
NHC = 22
EPS = 1e-6
NRING = 8
LAMBDA_INIT = 0.8 - 0.6 * math.exp(-0.3 * 0)
NEG = -30000.0
import os as _os
F_STORE_EARLY = _os.environ.get("F_STORE_EARLY", "0") == "1"
F_PREFILL = _os.environ.get("F_PREFILL", "1") == "1"
F_TRIG_LATE = _os.environ.get("F_TRIG_LATE", "1") == "1"

P_AN0, P_AN1, P_FN0, P_FN1, P_KVN = 0, 8, 16, 24, 32
P_QN0, P_KN0, P_SUB, P_KN1, P_QN1 = 40, 41, 42, 43, 44
P_SINK = 45
P_LAM = 61
P_INVF = 61 + 256
NPC = P_INVF + 1
C_ID, C_MCUR, C_MPREV, C_BONES, C_OMEAN, C_RPERM = 0, 128, 256, 384, 512, 640
NCC = 768


class Buf:
    __slots__ = ("name", "w", "rs", "rdma")

    def __init__(self, name):
        self.name = name
        self.w = None
        self.rs = {}
        self.rdma = []


class Op:
    __slots__ = ("eng", "fn", "deps", "need_inc", "is_dma", "dmabuf", "sem", "val")


class Sched:
    def __init__(self, nc, es):
        self.nc = nc
        self.es = es
        self.ops = []
        self.engs = {"pe": nc.tensor, "act": nc.scalar, "dve": nc.vector, "pool": nc.gpsimd, "sp": nc.sync}

    def add(self, eng, fn, reads=(), writes=(), dma=False, dmabuf=None, holds=()):
        op = Op()
        op.eng = eng
        op.fn = fn
        op.is_dma = dma
        op.dmabuf = dmabuf
        op.need_inc = False
        op.sem = None
        op.val = 0
        deps = {}

        def dep(d, raw):
            if d is op:
                return
            if (not d.is_dma) and (not dma) and d.eng == eng and eng == "pe":
                return
            deps[id(d)] = d

        for b in reads:
            if b.w is not None:
                dep(b.w, True)
        for b in writes:
            if b.w is not None:
                dep(b.w, False)
            for r in b.rs.values():
                dep(r, False)
            for r in b.rdma:
                dep(r, False)
        for b in list(reads) + list(holds):
            if dma:
                b.rdma.append(op)
            else:
                b.rs[eng] = op
        for b in writes:
            b.w = op
            b.rs = {}
            b.rdma = []
        op.deps = list(deps.values())
        for d in op.deps:
            d.need_inc = True
        self.ops.append(op)
        return op

    def emit(self):
        nc = self.nc
        sems = {}
        cnt = {}
        waited = {e: {} for e in self.engs}

        def getsem(key):
            if key not in sems:
                sems[key] = self.es.enter_context(nc.semaphore("s%d" % len(sems)))
                cnt[key] = 0
            return sems[key]

        for op in self.ops:
            eo = self.engs[op.eng]
            for d in op.deps:
                k = id(d.sem)
                if waited[op.eng].get(k, 0) < d.val:
                    eo.wait_ge(d.sem, d.val)
                    waited[op.eng][k] = d.val
            inst = op.fn(eo) if op.fn is not None else None
            if op.is_dma:
                key = ("dma", id(op.dmabuf), op.eng == "pool")
                s = getsem(key)
                insts = inst if isinstance(inst, (list, tuple)) else [inst]
                for i_ in insts:
                    cnt[key] += 16
                    i_.then_inc(s, 16)
                op.sem, op.val = s, cnt[key]
            elif op.need_inc:
                key = ("eng", op.eng)
                s = getsem(key)
                cnt[key] += 1
                inst.then_inc(s, 1)
                op.sem, op.val = s, cnt[key]
        self.stats = (len(self.ops), len(sems), {k[1]: v for k, v in cnt.items() if k[0] == "eng"})


def slab_table():
    sl = []
    for n in range(12):
        sl.append(("qkv%d" % n, [("da_w_qkv", 0, 0, 8, 256 * n, 256, 0, 256)]))
    for n in range(4):
        sl.append(("wo0_%d" % n, [("da_w_o", 0, 0, 8, 256 * n, 256, 0, 256)]))
    for l in range(2):
        if l == 1:
            for g2 in range(2):
                pcs = []
                for gl in range(2):
                    for dup in range(2):
                        pcs.append(("w_kv", None, 0, 8, (2 * g2 + gl) * 64, 64, gl * 128 + dup * 64, 256))
                sl.append(("kd%d" % g2, pcs))
            sl.append(("v1", [("w_kv", None, 0, 8, 256, 256, 0, 256)]))
            for n in range(4):
                sl.append(("q1_%d" % n, [("sw_w_q", 0, 0, 8, 256 * n, 256, 0, 256)]))
            for n in range(4):
                sl.append(("wo1_%d" % n, [("sw_w_o", 0, 0, 8, 256 * n, 256, 0, 256)]))
        for i in range(NHC):
            sl.append(("gu%d_%d" % (l, i), [("w_gate_up", l, 0, 8, 128 * i, 128, 0, 256),
                                            ("w_gate_up", l, 0, 8, HID + 128 * i, 128, 128, 256)]))
        for o in range(8):
            for hf in range(2):
                sl.append(("dn%d_%d_%d" % (l, o, hf), [("w_down", l, 11 * hf, 11, 128 * o, 128, 0, 128)]))
    return sl


def build_program(nseq=2, layers=(0, 1), n_mt=None, prologue=True):
    nc = bass.Bass("TRN2", target_bir_lowering=False)
    es = ExitStack()
    NTOK = nseq * S_LEN
    dr = {}
    dr["x"] = nc.dram_tensor("x", [NTOK, D], F32, kind="ExternalInput").ap()
    dr["pos"] = nc.dram_tensor("pos", [nseq, S_LEN], I32, kind="ExternalInput").ap()
    dr["da_w_qkv"] = nc.dram_tensor("da_w_qkv", [1, D, 3072], F32, kind="ExternalInput").ap()
    dr["da_w_o"] = nc.dram_tensor("da_w_o", [1, D, D], F32, kind="ExternalInput").ap()
    dr["w_gate_up"] = nc.dram_tensor("w_gate_up", [2, D, 2 * HID], F32, kind="ExternalInput").ap()
    dr["w_down"] = nc.dram_tensor("w_down", [2, HID, D], F32, kind="ExternalInput").ap()
    dr["w_kv"] = nc.dram_tensor("w_kv", [D, 512], F32, kind="ExternalInput").ap()
    dr["sw_w_q"] = nc.dram_tensor("sw_w_q", [1, D, D], F32, kind="ExternalInput").ap()
    dr["sw_w_o"] = nc.dram_tensor("sw_w_o", [1, D, D], F32, kind="ExternalInput").ap()
    dr["params"] = nc.dram_tensor("params", [128, NPC], F32, kind="ExternalInput").ap()
    dr["consts"] = nc.dram_tensor("consts", [128, NCC], F32, kind="ExternalInput").ap()
    out_d = nc.dram_tensor("out", [NTOK, D], F32, kind="ExternalOutput").ap()
    slabs = slab_table()
    NSLAB = len(slabs)
    slab_id = {s[0]: i for i, s in enumerate(slabs)}
    wscr = nc.dram_tensor("wscr", [NSLAB, 128, 2048], BF16).ap()

    S = Sched(nc, es)

    def sb(name, shape, dt):
        return es.enter_context(nc.sbuf_tensor(name, shape, dt))

    KT0 = sb("KT0", [128, 8, S_LEN], BF16)
    V0 = sb("V0", [128, 16, 8, 130], BF16)
    xT = sb("xT", [128, 8, T], F32)
    actA = sb("actA", [128, 8, T], BF16)
    actB = sb("actB", [128, 8, T], BF16)
    arena = sb("arena", [128, NHC, T], BF16)
    sqb = [sb("sqb%d" % i, [128, T], BF16) for i in range(2)]
    rstd = [sb("rstd%d" % i, [128, T], F32) for i in range(2)]
    rstdN = sb("rstdN", [128, T], F32)
    qnb = [sb("qnb%d" % i, [128, T], BF16) for i in range(2)]
    t1 = [sb("t1_%d" % i, [128, T], F32) for i in range(2)]
    t2 = [sb("t2_%d" % i, [128, T], F32) for i in range(2)]
    cosT = sb("cosT", [128, T], F32)
    sinT = sb("sinT", [128, T], F32)
    ti = sb("ti", [128, T], I32)
    PT = [sb("PT%d" % i, [128, 1024], BF16) for i in range(3)]
    rs = [sb("rs%d" % i, [128, 4, 2], F32) for i in range(2)]
    rs2 = [sb("rs2_%d" % i, [128, 4], F32) for i in range(2)]
    et = [sb("et%d" % i, [128, 4, 128], F32) for i in range(2)]
    eo = [sb("eo%d" % i, [128, 4, 128], F32) for i in range(2)]
    ssq = [sb("ssq%d" % i, [128, 4], F32) for i in range(2)]
    sg = [sb("sg%d" % i, [128, T], BF16) for i in range(2)]
    ring = [sb("ring%d" % i, [128, 2048], BF16) for i in range(NRING)]
    xs = sb("xs", [128, D], F32)
    ta = xs[:, 0:512]
    tb = xs[:, 512:1024]
    K1T = sb("K1T", [128, 4, 640], BF16)
    V1 = sb("V1", [128, 5, 4, 66], BF16)
    den = [sb("den%d" % i, [128, 4], F32) for i in range(2)]
    esink = sb("esink", [128, 16], F32)
    cbf = sb("cbf", [128, NCC], BF16)
    identf = sb("identf", [128, 128], F32)
    prm = sb("prm", [128, NPC], F32)
    epsb = sb("epsb", [128, 1], F32)
    eps128 = sb("eps128", [128, 1], F32)
    lamw = sb("lamw", [128, 2, 64], F32)
    lam2 = sb("lam2", [128, 2], F32)
    neglam = sb("neglam", [128, 1], F32)
    subg = sb("subg", [128, 1], F32)
    ps = es.enter_context(nc.psum_tensor("ps", [128, 8 * 512], F32))
    SBUF_LEFT = nc.sbuf_bytes_remaining

    def bank(b, c0=0, n=512):
        return ps[:, b * 512 + c0: b * 512 + c0 + n]

    B_bank = [Buf("bank%d" % i) for i in range(8)]
    B_KT0 = [[Buf("KT0_%d_%d" % (h, m)) for m in range(4)] for h in range(8)]
    B_V0 = [[Buf("V0_%d_%d" % (j, hp)) for hp in range(4)] for j in range(16)]
    B_xT = [Buf("xT%d" % c) for c in range(8)]
    B_actA = [Buf("actA%d" % c) for c in range(8)]
    B_actB = [Buf("actB%d" % c) for c in range(8)]
    B_ar = [Buf("ar%d" % c) for c in range(NHC)]
    B_obf = [Buf("obf%d" % h) for h in range(8)]
    B_sqb = [Buf("sqb%d" % i) for i in range(2)]
    B_rstd = [Buf("rstd%d" % i) for i in range(2)]
    B_rstdN = Buf("rstdN")
    B_qnb = [Buf("qnb%d" % i) for i in range(2)]
    B_t1 = [Buf("t1%d" % i) for i in range(2)]
    B_t2 = [Buf("t2%d" % i) for i in range(2)]
    B_cos, B_sin, B_ti = Buf("cos"), Buf("sin"), Buf("ti")
    B_PT = [Buf("PT%d" % i) for i in range(3)]
    B_rs = [Buf("rs%d" % i) for i in range(2)]
    B_rs2 = [Buf("rs2%d" % i) for i in range(2)]
    B_et = [Buf("et%d" % i) for i in range(2)]
    B_eo = [Buf("eo%d" % i) for i in range(2)]
    B_ssq = [Buf("ssq%d" % i) for i in range(2)]
    B_sg = [Buf("sg%d" % i) for i in range(2)]
    B_ring = [Buf("ring%d" % i) for i in range(NRING)]
    B_xs0, B_xs1 = Buf("xs0"), Buf("xs1")
    B_ta, B_tb = B_xs0, B_xs1
    B_K1T = [Buf("K1T%d" % g) for g in range(4)]
    B_K1Tprev = [Buf("K1Tp%d" % g) for g in range(4)]
    B_V1 = [Buf("V1_%d" % b) for b in range(5)]
    B_den = [Buf("den%d" % i) for i in range(2)]
    B_const = Buf("const")
    B_wscr = [Buf("wscr%d" % i) for i in range(NSLAB)]
    B_out = []

    cnt = {"ring": 0, "sqb": 0, "rstd": 0, "qnb": 0, "t1": 0, "t2": 0, "PT": 0, "ep": 0, "sg": 0, "den": 0}

    def rot(key, n):
        v = cnt[key] % n
        cnt[key] += 1
        return v

    S.add("pool", lambda e: e.dma_start(out=cbf[:], in_=dr["consts"]), writes=[B_const], dma=True, dmabuf=B_const)
    B_c2 = Buf("c2")
    S.add("sp", lambda e: [e.dma_start(out=identf[:], in_=dr["consts"][:, C_ID:C_ID + 128]),
                           e.dma_start(out=prm[:], in_=dr["params"])], writes=[B_c2], dma=True, dmabuf=B_c2)
    B_c3 = Buf("c3")
    S.add("pool", lambda e: e.memset(epsb[:], EPS), writes=[B_c3])
    S.add("pool", lambda e: e.memset(eps128[:], 128.0 * EPS), writes=[B_c3])
    S.add("pool", lambda e: e.memset(V0[:, :, :, 128:130], 1.0), writes=[b for r in B_V0 for b in r])
    S.add("pool", lambda e: e.memset(V1[:, :, :, 64:66], 1.0), writes=B_V1)
    B_l = Buf("lam")
    lamv = prm[:, P_LAM:P_LAM + 256].rearrange("p (a b d) -> p a b d", a=2, b=2)
    S.add("dve", lambda e: e.tensor_tensor(out=lamw[:], in0=lamv[:, :, 0, :], in1=lamv[:, :, 1, :], op=ALU.mult), reads=[B_c2], writes=[B_l])
    S.add("dve", lambda e: e.tensor_reduce(out=lam2[:], in_=lamw[:], axis=AX.X, op=ALU.add), reads=[B_l], writes=[B_l])
    S.add("act", lambda e: e.activation(out=lam2[:], in_=lam2[:], func=AF.Exp), reads=[B_l], writes=[B_l])
    S.add("dve", lambda e: e.tensor_tensor(out=neglam[:], in0=lam2[:, 1:2], in1=lam2[:, 0:1], op=ALU.subtract), reads=[B_l], writes=[B_l])
    S.add("dve", lambda e: e.tensor_scalar(out=neglam[:], in0=neglam[:], scalar1=-LAMBDA_INIT, scalar2=None, op0=ALU.add), reads=[B_l], writes=[B_l])
    S.add("dve", lambda e: e.tensor_scalar(out=subg[:], in0=prm[:, P_SUB:P_SUB + 1], scalar1=(1.0 - LAMBDA_INIT) * math.sqrt(128.0), scalar2=None, op0=ALU.mult), reads=[B_c2], writes=[B_l])
    S.add("act", lambda e: e.activation(out=esink[:], in_=prm[:, P_SINK:P_SINK + 16], func=AF.Exp), reads=[B_c2], writes=[B_l])
    CONST = [B_const, B_c2, B_c3, B_l]

    def cc(off, n=128):
        return cbf[:, off:off + n]

    def src_ap(piece):
        key, l, kc0, nkc, c0, ncol, d0, dstride = piece
        w = dr[key]
        if l is not None:
            w = w[l]
        return w.rearrange("(kc p) n -> p kc n", p=128)[:, kc0:kc0 + nkc, c0:c0 + ncol]

    def dst_ap(slot, piece):
        key, l, kc0, nkc, c0, ncol, d0, dstride = piece
        v = ring[slot][:, 0:nkc * dstride].rearrange("p (kc n) -> p kc n", n=dstride)
        return v[:, :, d0:d0 + ncol]

    LOOK = 6

    class SlabStream:
        def __init__(self, order, n_tiles):
            self.order = order
            self.idx = {nm: i for i, nm in enumerate(order)}
            self.n = len(order)
            self.total = n_tiles * self.n
            self.emitted = 0
            self.slots = {}
            self.tile = 0

        def _emit(self, gi):
            t, k = divmod(gi, self.n)
            name = self.order[k]
            si = slab_id[name]
            slot = rot("ring", NRING)
            if t == 0:
                pieces = slabs[si][1]

                def ld(e):
                    return [e.dma_start(out=dst_ap(slot, p), in_=src_ap(p)) for p in pieces]

                S.add("pool", ld, writes=[B_ring[slot]], dma=True, dmabuf=B_ring[slot])
                S.add("sp", lambda e: e.dma_start(out=wscr[si], in_=ring[slot][:]),
                      reads=[B_ring[slot]], writes=[B_wscr[si]], dma=True, dmabuf=B_ring[slot])
            else:
                S.add("sp", lambda e: e.dma_start(out=ring[slot][:], in_=wscr[si]),
                      reads=[B_wscr[si]], writes=[B_ring[slot]], dma=True, dmabuf=B_ring[slot])
            self.slots[gi] = slot

        def get(self, name):
            gi = self.tile * self.n + self.idx[name]
            upto = min(self.total, gi + 1 + LOOK)
            while self.emitted < upto:
                self._emit(self.emitted)
                self.emitted += 1
            return self.slots[gi]

        fetch = get

    order = []
    if 0 in layers:
        order += ["qkv%d" % n for n in range(12)] + ["wo0_%d" % n for n in range(4)]
        order += ["gu0_%d" % i for i in range(NHC)] + ["dn0_%d_%d" % (o, hf) for o in range(8) for hf in range(2)]
    if 1 in layers:
        order += ["kd0", "kd1"] + ["q1_%d" % n for n in range(4)] + ["v1"] + ["wo1_%d" % n for n in range(4)]
        order += ["gu1_%d" % i for i in range(NHC)] + ["dn1_%d_%d" % (o, hf) for o in range(8) for hf in range(2)]
    SS = SlabStream(order, nseq * (S_LEN // T if n_mt is None else n_mt))

    def slab_view(slot, stride):
        return ring[slot][:].rearrange("p (kc n) -> p kc n", n=stride)

    def rms_stats(ssbank):
        for c in range(8):
            q = rot("sqb", 2)
            S.add("act", lambda e, c=c, q=q: e.activation(out=sqb[q][:], in_=xT[:, c, :], func=AF.Square),
                  reads=[B_xT[c]], writes=[B_sqb[q]])
            S.add("pe", lambda e, c=c, q=q: e.matmul(bank(ssbank), lhsT=cc(C_OMEAN), rhs=sqb[q][:], start=(c == 0), stop=(c == 7)),
                  reads=[B_sqb[q]] + CONST, writes=[B_bank[ssbank]])
        S.add("act", lambda e: e.activation(out=rstdN[:], in_=bank(ssbank), func=AF.Ln, bias=epsb[:, 0:1], scale=1.0),
              reads=[B_bank[ssbank]] + CONST, writes=[B_rstdN])
        S.add("act", lambda e: e.activation(out=rstdN[:], in_=rstdN[:], func=AF.Exp, scale=-0.5),
              reads=[B_rstdN], writes=[B_rstdN])
        return 0

    def apply_norm(r, gcol, dst, B_dst, defer=False):
        ops = []
        for c in range(8):
            def one(c=c):
                S.add("dve", lambda e: e.scalar_tensor_tensor(out=dst[:, c, :], in0=xT[:, c, :], scalar=prm[:, gcol + c:gcol + c + 1],
                                                              in1=rstdN[:], op0=ALU.mult, op1=ALU.mult),
                      reads=[B_xT[c], B_rstdN] + CONST, writes=[B_dst[c]])
            ops.append(one)
        if defer:
            return ops
        for o_ in ops:
            o_()

    def proj_fm(slot, colo, act, B_act, obank, stride=256, split=False):
        sv = slab_view(slot, stride)
        if split:
            for kc in range(8):
                S.add("pe", lambda e, kc=kc: e.matmul(bank(obank), lhsT=sv[:, kc, colo:colo + 128], rhs=act[:, kc, :], start=(kc == 0), stop=(kc == 7)),
                      reads=[B_ring[slot], B_act[kc]], writes=[B_bank[obank]])
            return

        def f(e):
            for kc in range(8):
                i_ = e.matmul(bank(obank), lhsT=sv[:, kc, colo:colo + 128], rhs=act[:, kc, :], start=(kc == 0), stop=(kc == 7))
            return i_

        S.add("pe", f, reads=[B_ring[slot]] + B_act, writes=[B_bank[obank]])

    def qk_pipeline(jobs, fillers=(), prefill=()):
        n = len(jobs)
        RAW, SSB, ROT = (0, 1, 2), (3, 4), (5, 6)
        stt = [dict() for _ in jobs]

        def A(i):
            name, colo, act, B_act_, gcol, dst, B_dst = jobs[i]
            slot = SS.get(name)
            rb = RAW[i % 3]
            proj_fm(slot, colo, act, B_act_, rb, split=(i == 0))
            q = rot("sqb", 2)
            S.add("act", lambda e: e.activation(out=sqb[q][:], in_=bank(rb), func=AF.Square), reads=[B_bank[rb]], writes=[B_sqb[q]])
            stt[i].update(rb=rb, q=q)

        def Bst(i):
            name, colo, act, B_act_, gcol, dst, B_dst = jobs[i]
            rb, q = stt[i]["rb"], stt[i]["q"]
            sbk = SSB[i % 2]
            S.add("pe", lambda e: e.matmul(bank(sbk), lhsT=cc(C_BONES), rhs=sqb[q][:], start=True, stop=True),
                  reads=[B_sqb[q]] + CONST, writes=[B_bank[sbk]])
            r = rot("rstd", 2)
            S.add("act", lambda e: e.activation(out=rstd[r][:], in_=bank(sbk), func=AF.Ln, bias=epsb[:, 0:1], scale=1.0),
                  reads=[B_bank[sbk]] + CONST, writes=[B_rstd[r]])
            S.add("act", lambda e: e.activation(out=rstd[r][:], in_=rstd[r][:], func=AF.Exp, scale=-0.5), reads=[B_rstd[r]], writes=[B_rstd[r]])
            nn = rot("qnb", 2)
            S.add("dve", lambda e: e.scalar_tensor_tensor(out=qnb[nn][:], in0=bank(rb), scalar=prm[:, gcol:gcol + 1], in1=rstd[r][:],
                                                          op0=ALU.mult, op1=ALU.mult),
                  reads=[B_bank[rb], B_rstd[r]] + CONST, writes=[B_qnb[nn]])
            stt[i].update(nn=nn)

        def Cst(i):
            name, colo, act, B_act_, gcol, dst, B_dst = jobs[i]
            nn = stt[i]["nn"]
            rtb = ROT[i % 2]
            S.add("pe", lambda e: e.matmul(bank(rtb), lhsT=cc(C_RPERM), rhs=qnb[nn][:], start=True, stop=True),
                  reads=[B_qnb[nn]] + CONST, writes=[B_bank[rtb]])
            a = rot("t1", 2)
            S.add("pool", lambda e: e.tensor_tensor(out=t1[a][:], in0=qnb[nn][:], in1=cosT[:], op=ALU.mult), reads=[B_qnb[nn], B_cos], writes=[B_t1[a]])
            b = rot("t2", 2)
            S.add("dve", lambda e: e.tensor_tensor(out=t2[b][:], in0=bank(rtb), in1=sinT[:], op=ALU.mult), reads=[B_bank[rtb], B_sin], writes=[B_t2[b]])
            S.add("pool" if i % 2 else "dve", lambda e: e.tensor_tensor(out=dst, in0=t1[a][:], in1=t2[b][:], op=ALU.add), reads=[B_t1[a], B_t2[b]], writes=B_dst)

        fillers = list(fillers)
        per = (len(fillers) + 2) // 3
        prefill = list(prefill)
        for step in range(n + 2):
            for _ in range(2):
                if prefill:
                    prefill.pop(0)()
            if step < n:
                A(step)
            if 0 <= step - 1 < n:
                Bst(step - 1)
            if 0 <= step - 2 < n:
                Cst(step - 2)
            if step >= n - 1:
                for _ in range(per):
                    if fillers:
                        fillers.pop(0)()
        while fillers:
            fillers.pop(0)()

    def residual_add(c, obank):
        S.add("dve", lambda e: e.tensor_tensor(out=xT[:, c, :], in0=xT[:, c, :], in1=bank(obank), op=ALU.add),
              reads=[B_xT[c], B_bank[obank]], writes=[B_xT[c]])

    nacc = {"pend": None}

    def norm_acc_chunk(c):
        q = rot("sqb", 2)
        S.add("act", lambda e: e.activation(out=sqb[q][:], in_=xT[:, c, :], func=AF.Square), reads=[B_xT[c]], writes=[B_sqb[q]])
        norm_acc_flush(False)
        nacc["pend"] = (c, q)

    def norm_acc_flush(last):
        if nacc["pend"] is None:
            return
        c, q = nacc["pend"]
        S.add("pe", lambda e: e.matmul(bank(7), lhsT=cc(C_OMEAN), rhs=sqb[q][:], start=(c == 0), stop=(c == 7)),
              reads=[B_sqb[q]] + CONST, writes=[B_bank[7]])
        nacc["pend"] = None

    def norm_acc_finish():
        norm_acc_flush(True)
        S.add("act", lambda e: e.activation(out=rstdN[:], in_=bank(7), func=AF.Ln, bias=epsb[:, 0:1], scale=1.0),
              reads=[B_bank[7]] + CONST, writes=[B_rstdN])
        S.add("act", lambda e: e.activation(out=rstdN[:], in_=rstdN[:], func=AF.Exp, scale=-0.5), reads=[B_rstdN], writes=[B_rstdN])
        return 0

    def trig_dve(seq, tok0):
        S.add("sp", lambda e: e.dma_start(out=ti[:], in_=dr["pos"][seq:seq + 1, tok0:tok0 + T].partition_broadcast(128)),
              writes=[B_ti], dma=True, dmabuf=B_ti)
        S.add("dve", lambda e: e.tensor_copy(out=ta[:], in_=ti[:]), reads=[B_ti], writes=[B_ta])
        S.add("dve", lambda e: e.tensor_scalar(out=ta[:], in0=ta[:], scalar1=prm[:, P_INVF:P_INVF + 1], scalar2=float(1.0 / (2 * np.pi)),
                                               op0=ALU.mult, op1=ALU.mult), reads=[B_ta] + CONST, writes=[B_ta])
        for which, shift, dstt, B_d in (("s", 0.0, sinT, B_sin), ("c", 0.25, cosT, B_cos)):
            S.add("dve", lambda e, shift=shift: e.tensor_scalar(out=tb[:], in0=ta[:], scalar1=shift, scalar2=None, op0=ALU.add),
                  reads=[B_ta], writes=[B_tb])
            S.add("dve", lambda e: e.tensor_copy(out=ti[:], in_=tb[:]), reads=[B_tb], writes=[B_ti])
            S.add("dve", lambda e, dstt=dstt: e.tensor_copy(out=dstt[:], in_=ti[:]), reads=[B_ti], writes=[B_d])
            S.add("dve", lambda e, dstt=dstt: e.tensor_tensor(out=dstt[:], in0=tb[:], in1=dstt[:], op=ALU.subtract), reads=[B_tb, B_d], writes=[B_d])
            S.add("dve", lambda e, dstt=dstt: e.tensor_single_scalar(out=tb[:], in_=dstt[:], scalar=0.5, op=ALU.is_gt), reads=[B_d], writes=[B_tb])
            S.add("dve", lambda e, dstt=dstt: e.tensor_tensor(out=dstt[:], in0=dstt[:], in1=tb[:], op=ALU.subtract), reads=[B_tb, B_d], writes=[B_d])
            S.add("dve", lambda e, dstt=dstt: e.tensor_single_scalar(out=tb[:], in_=dstt[:], scalar=-0.5, op=ALU.is_lt), reads=[B_d], writes=[B_tb])
            S.add("dve", lambda e, dstt=dstt: e.tensor_tensor(out=dstt[:], in0=dstt[:], in1=tb[:], op=ALU.add), reads=[B_tb, B_d], writes=[B_d])

    def trig_act():
        for dstt, B_d in ((sinT, B_sin), (cosT, B_cos)):
            S.add("act", lambda e, dstt=dstt: e.activation(out=dstt[:], in_=dstt[:], func=AF.Sin, scale=float(2 * np.pi)), reads=[B_d], writes=[B_d])

    def trig_tables(seq, tok0):
        trig_dve(seq, tok0)
        trig_act()

    def stage_halves(st):
        if st < 2:
            v = actA[:, 4 * st:4 * st + 4, :].rearrange("p a b -> p (a b)").bitcast(F32)
            return [(v[:, 0:512], B_actA[4 * st:4 * st + 2]), (v[:, 512:1024], B_actA[4 * st + 2:4 * st + 4])]
        tt, BB = (t1, B_t1) if st == 2 else (t2, B_t2)
        return [(tt[0][:], [BB[0]]), (tt[1][:], [BB[1]])]

    def issue_loads(g0):
        for st in range(NST):
            r0 = g0 + st * 128
            for half, (ap_, bufs) in enumerate(stage_halves(st)):
                S.add("sp", lambda e, r0=r0, half=half, ap_=ap_: e.dma_start(out=ap_, in_=dr["x"][r0:r0 + 128, half * 512:(half + 1) * 512]),
                      writes=bufs, dma=True, dmabuf=bufs[0])

    def load_x(g0):
        for half in range(2):
            for st in range(NST):
                ap_, bufs = stage_halves(st)[half]
                bk = (2 * st + half) % 4

                def tr(e, ap_=ap_, bk=bk):
                    for c4 in range(4):
                        i_ = e.transpose(out=bank(bk, c4 * 128, 128), in_=ap_[:, c4 * 128:(c4 + 1) * 128], identity=identf[:])
                    return i_

                S.add("pe", tr, reads=bufs + CONST, writes=[B_bank[bk]])
                if half == 0:
                    S.add("act", lambda e, half=half, bk=bk, st=st: e.activation(
                        out=xT[:, half * 4:half * 4 + 4, st * 128:(st + 1) * 128],
                        in_=bank(bk).rearrange("p (c t) -> p c t", c=4), func=AF.Copy),
                        reads=[B_bank[bk]], writes=B_xT[half * 4:half * 4 + 4])
                else:
                    S.add("dve", lambda e, half=half, bk=bk, st=st: e.tensor_copy(
                        out=xT[:, half * 4:half * 4 + 4, st * 128:(st + 1) * 128],
                        in_=bank(bk).rearrange("p (c t) -> p c t", c=4)),
                        reads=[B_bank[bk]], writes=B_xT[half * 4:half * 4 + 4])
            for c in range(half * 4, half * 4 + 4):
                norm_acc_chunk(c)
        return norm_acc_finish()

    stg = [(xs[:, 0:512], B_xs0), (xs[:, 512:1024], B_xs1), (PT[0][:].bitcast(F32), B_PT[0]), (PT[1][:].bitcast(F32), B_PT[1])]
    stq = {"q": 0}

    def store_groups(g0, half):
        for st in range(NST):
            r0 = g0 + st * 128
            bk = (0, 1, 4, 5)[stq["q"] % 4]
            o_ap, o_b = stg[stq["q"] % 4]
            stq["q"] += 1

            def tr(e, half=half, bk=bk, st=st):
                for c4 in range(4):
                    c = half * 4 + c4
                    i_ = e.transpose(out=bank(bk, c4 * 128, 128), in_=xT[:, c, st * 128:(st + 1) * 128], identity=identf[:])
                return i_

            S.add("pe", tr, reads=B_xT[half * 4:half * 4 + 4] + CONST, writes=[B_bank[bk]])
            S.add("dve", lambda e, bk=bk, o_ap=o_ap: e.tensor_copy(out=o_ap, in_=bank(bk)), reads=[B_bank[bk]], writes=[o_b])
            bo = Buf("out")
            B_out.append(bo)
            S.add("sp", lambda e, r0=r0, half=half, o_ap=o_ap: e.dma_start(out=out_d[r0:r0 + 128, half * 512:(half + 1) * 512], in_=o_ap),
                  reads=[o_b], writes=[bo], dma=True, dmabuf=o_b)

    def transposes_to(act, B_act, scale_ap):
        for st in range(NST):
            bk = st % 2
            pb = bank(bk).bitcast(BF16)

            def tr(e, st=st, pb=pb):
                for c in range(8):
                    i_ = e.transpose(out=pb[:, c * 128:(c + 1) * 128], in_=arena[:, 2 * st + c // 4, (c % 4) * 128:(c % 4) * 128 + 128],
                                     identity=cc(C_ID))
                return i_

            S.add("pe", tr, reads=[B_ar[2 * st], B_ar[2 * st + 1]] + CONST, writes=[B_bank[bk]])
            if st % 2 == 1:
                if scale_ap is not None:
                    S.add("dve", lambda e, st=st, pb=pb: e.tensor_scalar(out=act[:, :, st * 128:(st + 1) * 128], in0=pb.rearrange("p (c t) -> p c t", c=8),
                                                                        scalar1=scale_ap, scalar2=None, op0=ALU.mult), reads=[B_bank[bk]] + CONST, writes=B_act)
                else:
                    S.add("dve", lambda e, st=st, pb=pb: e.tensor_copy(out=act[:, :, st * 128:(st + 1) * 128], in_=pb.rearrange("p (c t) -> p c t", c=8)),
                          reads=[B_bank[bk]], writes=B_act)
            elif scale_ap is not None:
                S.add("act", lambda e, st=st, pb=pb: e.activation(out=act[:, :, st * 128:(st + 1) * 128], in_=pb.rearrange("p (c t) -> p c t", c=8),
                                                                 func=AF.Copy, scale=scale_ap), reads=[B_bank[bk]] + CONST, writes=B_act)
            else:
                S.add("act", lambda e, st=st, pb=pb: e.activation(out=act[:, :, st * 128:(st + 1) * 128], in_=pb.rearrange("p (c t) -> p c t", c=8),
                                                                 func=AF.Copy), reads=[B_bank[bk]], writes=B_act)

    def out_proj(prefix, act, B_act, split_first=False):
        for n in range(4):
            slot = SS.fetch("%s_%d" % (prefix, n))
            for cl in range(2):
                o = 2 * n + cl
                ob = 2 + (o % 2)
                proj_fm(slot, cl * 128, act, B_act, ob, split=(split_first and o == 0))
                residual_add(o, ob)
                norm_acc_chunk(o)
        return norm_acc_finish()

    def ffn(l, gcol, r, hook2=None, acc=False, store_g0=None):
        apply_norm(r, gcol, actB, B_actB)
        if hook2 is not None:
            hook2[0]()
        for i in range(NHC):
            slot = SS.fetch("gu%d_%d" % (l, i))
            gb = 4 + (i % 2)
            ub = 6 + (i % 2)
            proj_fm(slot, 0, actB, B_actB, gb, split=(i == 0))
            proj_fm(slot, 128, actB, B_actB, ub)
            s_ = rot("sg", 2)
            S.add("act", lambda e, gb=gb, s_=s_: e.activation(out=sg[s_][:], in_=bank(gb), func=AF.Silu), reads=[B_bank[gb]], writes=[B_sg[s_]])
            S.add("dve", lambda e, ub=ub, s_=s_, i=i: e.tensor_tensor(out=arena[:, i, :], in0=sg[s_][:], in1=bank(ub), op=ALU.mult),
                  reads=[B_sg[s_], B_bank[ub]], writes=[B_ar[i]])
            if i == (8 if F_TRIG_LATE else 0) and hook2 is not None:
                hook2[1]()
        for o in range(8):
            slots = [SS.fetch("dn%d_%d_%d" % (l, o, hf)) for hf in range(2)]
            ob = 2 + (o % 2)

            def f(e, slots=slots, ob=ob):
                for kc in range(NHC):
                    sv = slab_view(slots[kc // 11], 128)
                    i_ = e.matmul(bank(ob), lhsT=sv[:, kc % 11, 0:128], rhs=arena[:, kc, :], start=(kc == 0), stop=(kc == NHC - 1))
                return i_

            S.add("pe", f, reads=[B_ring[slots[0]], B_ring[slots[1]]] + B_ar, writes=[B_bank[ob]])
            residual_add(o, ob)
            if acc:
                norm_acc_chunk(o)
            if store_g0 is not None and o == 4 and F_STORE_EARLY:
                store_groups(store_g0, 0)
        if store_g0 is not None:
            if not F_STORE_EARLY:
                store_groups(store_g0, 0)
            store_groups(store_g0, 1)
        return norm_acc_finish() if acc else None

    def layer0(mt, hook=None, r0=None, store_g0=None):
        tok0 = mt * T
        r = r0 if r0 is not None else rms_stats(7)
        apply_norm(r, P_AN0, actA, B_actA)
        jobs = []
        for n in range(8):
            for hl in range(2):
                h = (2 * n + hl) % 8
                if n < 4:
                    jobs.append(("qkv%d" % n, hl * 128, actA, B_actA, P_QN0, arena[:, 8 + h, :], [B_ar[8 + h]]))
                else:
                    jobs.append(("qkv%d" % n, hl * 128, actA, B_actA, P_KN0, KT0[:, h, tok0:tok0 + T], [B_KT0[h][mt]]))
        fillers = []
        for n in range(4):
            for st in range(NST):
                def vgrp(n=n, st=st):
                    slot = SS.get("qkv%d" % (8 + n))
                    sv = slab_view(slot, 256)
                    vb = (7, 1)[st % 2]
                    j = 4 * mt + st

                    def f(e):
                        for kc in range(8):
                            i_ = e.matmul(bank(vb, 0, 256), lhsT=actA[:, kc, st * 128:(st + 1) * 128], rhs=sv[:, kc, 0:256], start=(kc == 0), stop=(kc == 7))
                        return i_

                    S.add("pe", f, reads=[B_ring[slot]] + B_actA, writes=[B_bank[vb]])
                    S.add("act", lambda e: e.activation(out=V0[:, j, 2 * n:2 * n + 2, 0:128],
                                                        in_=bank(vb, 0, 256).rearrange("p (h e) -> p h e", h=2), func=AF.Copy),
                          reads=[B_bank[vb]], writes=[B_V0[j][n]])
                fillers.append(vgrp)
        qk_pipeline(jobs, fillers)
        nch = 4 * mt + 4
        pending = []
        pre = None
        for h in range(8):
            def s_op(h, j):
                qTh = arena[:, 8 + h, :]
                st0 = max(0, j - 4 * mt)
                N = (NST - st0) * 128
                sbuf_i = j % 2
                b0 = 2 * sbuf_i
                diag = j >= 4 * mt

                def f(e):
                    i_ = e.matmul(bank(b0, 0, N), lhsT=KT0[0:64, h, j * 128:(j + 1) * 128], rhs=qTh[0:64, st0 * 128:T], start=True, stop=not diag)
                    i_ = e.matmul(bank(b0 + 1, 0, N), lhsT=KT0[64:128, h, j * 128:(j + 1) * 128], rhs=qTh[64:128, st0 * 128:T], start=True, stop=not diag,
                                  tile_position=(64, 0))
                    if diag:
                        i_ = e.matmul(bank(b0, 0, 128), lhsT=cc(C_ID), rhs=cc(C_MCUR), start=False, stop=True)
                        i_ = e.matmul(bank(b0 + 1, 0, 128), lhsT=cc(C_ID), rhs=cc(C_MCUR), start=False, stop=True)
                    return i_

                S.add("pe", f, reads=[B_KT0[h][j // 4], B_ar[8 + h]] + CONST, writes=[B_bank[b0], B_bank[b0 + 1]])
                p = rot("PT", 3)
                S.add("act", lambda e: e.activation(out=PT[p][:].rearrange("p (c n) -> p c n", c=2)[:, :, 0:N],
                                                    in_=ps[:, b0 * 512:(b0 + 2) * 512].rearrange("p (c n) -> p c n", c=2)[:, :, 0:N],
                                                    func=AF.Exp, scale=0.125),
                      reads=[B_bank[b0], B_bank[b0 + 1]], writes=[B_PT[p]])
                return (j, st0, p)

            def av_op(h, j, st0, p):
                def f(e):
                    for st in range(st0, NST):
                        for c in range(2):
                            i_ = e.matmul(bank(4 + st, c * 256, 129), lhsT=PT[p][:, c * 512 + (st - st0) * 128: c * 512 + (st - st0) * 128 + 128],
                                          rhs=V0[:, j, h, 0:129], start=(j == 0 and c == 0), stop=(j == 4 * mt + st), skip_group_check=True)
                    return i_

                S.add("pe", f, reads=[B_PT[p], B_V0[j][h // 2]], writes=[B_bank[4 + st] for st in range(st0, NST)])

            prev = None
            for j in range(nch):
                if j == 0 and pre is not None:
                    cur, pre = pre, None
                else:
                    cur = s_op(h, j)
                if prev is not None:
                    av_op(h, *prev)
                prev = cur
                if j == 2 and pending:
                    pending.pop(0)()
            if h + 1 < 8:
                pre = s_op(h + 1, 0)
            av_op(h, *prev)
            k = rot("ep", 2)
            psv = ps[:, 4 * 512:8 * 512].rearrange("p (s c n) -> p s c n", s=4, c=2)
            S.add("dve", lambda e, k=k: e.reciprocal(out=rs[k][:], in_=psv[:, :, :, 128]), reads=B_bank[4:8], writes=[B_rs[k]])
            S.add("dve", lambda e, k=k: e.tensor_tensor(out=eo[k][:], in0=psv[:, :, 0, 0:128], in1=rs[k][:, :, 0:1].to_broadcast([128, 4, 128]), op=ALU.mult),
                  reads=B_bank[4:8] + [B_rs[k]], writes=[B_eo[k]])
            S.add("dve", lambda e, k=k: e.tensor_tensor(out=et[k][:], in0=psv[:, :, 1, 0:128], in1=rs[k][:, :, 1:2].to_broadcast([128, 4, 128]), op=ALU.mult),
                  reads=B_bank[4:8] + [B_rs[k]], writes=[B_et[k]])
            S.add("dve", lambda e, k=k: e.scalar_tensor_tensor(out=eo[k][:], in0=et[k][:], scalar=neglam[:, 0:1], in1=eo[k][:], op0=ALU.mult, op1=ALU.add),
                  reads=[B_eo[k], B_et[k]] + CONST, writes=[B_eo[k]])
            S.add("dve", lambda e, k=k: e.tensor_tensor(out=et[k][:], in0=eo[k][:], in1=eo[k][:], op=ALU.mult), reads=[B_eo[k]], writes=[B_et[k]])
            S.add("dve", lambda e, k=k: e.tensor_reduce(out=ssq[k][:], in_=et[k][:], axis=AX.X, op=ALU.add), reads=[B_et[k]], writes=[B_ssq[k]])
            obv = arena[:, 0:8, :].rearrange("p (s two) (q e) -> p s two q e", two=2, e=128)[:, :, h // 4, h % 4, :]

            def tail(k=k, obv=obv, h=h):
                S.add("act", lambda e: e.activation(out=ssq[k][:], in_=ssq[k][:], func=AF.Ln, bias=eps128[:, 0:1], scale=1.0),
                      reads=[B_ssq[k]] + CONST, writes=[B_ssq[k]])
                S.add("act", lambda e: e.activation(out=ssq[k][:], in_=ssq[k][:], func=AF.Exp, scale=-0.5), reads=[B_ssq[k]], writes=[B_ssq[k]])
                S.add("dve", lambda e: e.tensor_tensor(out=obv, in0=eo[k][:], in1=ssq[k][:].unsqueeze(2).to_broadcast([128, 4, 128]), op=ALU.mult),
                      reads=[B_eo[k], B_ssq[k]], writes=[B_obf[h]] + [B_ar[2 * st_ + h // 4] for st_ in range(NST)])

            pending.append(tail)
        while pending:
            pending.pop(0)()
        for c in range(8):
            bk = c % 4
            pb = bank(bk).bitcast(BF16)[:, 0:512]

            def tr(e, c=c, pb=pb):
                for st in range(NST):
                    i_ = e.transpose(out=pb[:, st * 128:(st + 1) * 128], in_=arena[:, 2 * st + c // 4, (c % 4) * 128:(c % 4) * 128 + 128], identity=cc(C_ID))
                return i_

            S.add("pe", tr, reads=[B_obf[c]] + CONST, writes=[B_bank[bk]], holds=[B_ar[2 * st_ + c // 4] for st_ in range(NST)])
            if c % 2 == 0:
                S.add("act", lambda e, c=c, pb=pb: e.activation(out=actA[:, c, :], in_=pb, func=AF.Copy, scale=subg[:, 0:1]),
                      reads=[B_bank[bk]] + CONST, writes=[B_actA[c]])
            else:
                S.add("dve", lambda e, c=c, pb=pb: e.tensor_scalar(out=actA[:, c, :], in0=pb, scalar1=subg[:, 0:1], scalar2=None, op0=ALU.mult),
                      reads=[B_bank[bk]] + CONST, writes=[B_actA[c]])
        r = out_proj("wo0", actA, B_actA, split_first=True)
        if hook is not None:
            hook[0]()
        return ffn(0, P_FN0, r, hook[1] if hook is not None else None, acc=(1 in layers), store_g0=store_g0)

    def layer1(mt, hook=None, r=None, store_g0=None):
        tok0 = mt * T
        if r is None:
            r = rms_stats(7)
        apply_norm(r, P_KVN, actA, B_actA)
        pre = apply_norm(r, P_AN1, actB, B_actB, defer=True)
        if not F_PREFILL:
            for o_ in pre:
                o_()
            pre = []
        jobs = []
        for g2 in range(2):
            for gl in range(2):
                g = 2 * g2 + gl
                jobs.append(("kd%d" % g2, gl * 128, actA, B_actA, P_KN1, K1T[:, g, 128:640], [B_K1T[g]]))
        for n in range(4):
            for cl in range(2):
                ci = 2 * n + cl
                jobs.append(("q1_%d" % n, cl * 128, actB, B_actB, P_QN1, arena[:, 8 + ci, :], [B_ar[8 + ci]]))
        fillers = []
        for st in range(NST):
            def vgrp(st=st):
                slot = SS.get("v1")
                sv = slab_view(slot, 256)
                vb = (7, 0)[st % 2]

                def f(e):
                    for kc in range(8):
                        i_ = e.matmul(bank(vb, 0, 256), lhsT=actA[:, kc, st * 128:(st + 1) * 128], rhs=sv[:, kc, 0:256], start=(kc == 0), stop=(kc == 7))
                    return i_

                S.add("pe", f, reads=[B_ring[slot]] + B_actA, writes=[B_bank[vb]])
                S.add("act", lambda e: e.activation(out=V1[:, 1 + st, :, 0:64], in_=bank(vb, 0, 256).rearrange("p (g d) -> p g d", g=4), func=AF.Copy),
                      reads=[B_bank[vb]], writes=[B_V1[1 + st]])
            fillers.append(vgrp)
        qk_pipeline(jobs, fillers, pre)
        units = []
        for st in range(NST):
            iseq = 4 * mt + st
            blks = []
            if iseq > 0:
                blks.append(("prev", st * 128, st, C_MPREV))
            blks.append(("cur", 128 + st * 128, st + 1, C_MCUR))
            for g in range(4):
                for bi, blk in enumerate(blks):
                    units.append((st, g, bi, len(blks), blk))

        def s_unit(ui):
            st, g, bi, nb, (bn, kc0, vblk, moff) = units[ui]
            b0 = 2 * (ui % 2)
            kbufs = [B_K1T[g]] + ([B_K1Tprev[g]] if (bn == "prev" and st == 0) else [])

            def f(e):
                e.matmul(bank(b0, 0, 256).rearrange("p (a b) -> p a b", a=2), lhsT=K1T[0:64, g, kc0:kc0 + 128],
                         rhs=arena[0:64, 8 + 2 * g:8 + 2 * g + 2, st * 128:(st + 1) * 128], start=True, stop=False)
                e.matmul(bank(b0 + 1, 0, 256).rearrange("p (a b) -> p a b", a=2), lhsT=K1T[64:128, g, kc0:kc0 + 128],
                         rhs=arena[64:128, 8 + 2 * g:8 + 2 * g + 2, st * 128:(st + 1) * 128], start=True, stop=False, tile_position=(64, 0))
                e.matmul(bank(b0, 0, 256).rearrange("p (a b) -> p a b", a=2), lhsT=cc(C_ID),
                         rhs=cc(moff).unsqueeze(1).to_broadcast([128, 2, 128]), start=False, stop=True)
                return e.matmul(bank(b0 + 1, 0, 256).rearrange("p (a b) -> p a b", a=2), lhsT=cc(C_ID),
                                rhs=cc(moff).unsqueeze(1).to_broadcast([128, 2, 128]), start=False, stop=True)

            S.add("pe", f, reads=kbufs + [B_ar[8 + 2 * g], B_ar[8 + 2 * g + 1]] + CONST, writes=[B_bank[b0], B_bank[b0 + 1]])
            p = rot("PT", 3)
            S.add("act", lambda e: e.activation(out=PT[p][:].rearrange("p (c n) -> p c n", c=2)[:, :, 0:256],
                                                in_=ps[:, b0 * 512:(b0 + 2) * 512].rearrange("p (c n) -> p c n", c=2)[:, :, 0:256],
                                                func=AF.Exp, scale=0.125),
                  reads=[B_bank[b0], B_bank[b0 + 1]], writes=[B_PT[p]])
            return p

        def av_unit(ui, p):
            st, g, bi, nb, (bn, kc0, vblk, moff) = units[ui]

            def av(e):
                first = True
                for half in range(2):
                    for idx in range(2):
                        hd = 4 * g + 2 * idx + half
                        i_ = e.matmul(bank(4 + hd // 4, (hd % 4) * 68, 65), lhsT=PT[p][:, half * 512 + idx * 128: half * 512 + idx * 128 + 128],
                                      rhs=V1[:, vblk, g, 0:65], start=(bi == 0 and first), stop=(bi == nb - 1), skip_group_check=True)
                        first = False
                return i_

            S.add("pe", av, reads=[B_PT[p], B_V1[vblk]], writes=[B_bank[4 + g]])
            if bi == nb - 1:
                epilogue1(st, g)

        def epilogue1(st, b):
            if True:
                k = rot("den", 2)
                ov = bank(4 + b, 0, 272).rearrange("p (h e) -> p h e", e=68)
                S.add("dve", lambda e, k=k, ov=ov, b=b: e.tensor_tensor(out=den[k][:], in0=ov[:, :, 64], in1=esink[:, 4 * b:4 * b + 4], op=ALU.add),
                      reads=[B_bank[4 + b]] + CONST, writes=[B_den[k]])
                S.add("dve", lambda e, k=k: e.reciprocal(out=den[k][:], in_=den[k][:]), reads=[B_den[k]], writes=[B_den[k]])
                dstv = arena[:, 2 * st + b // 2, (b % 2) * 256:(b % 2) * 256 + 256].rearrange("p (h d) -> p h d", h=4)
                S.add("dve", lambda e, k=k, ov=ov, dstv=dstv: e.tensor_tensor(out=dstv, in0=ov[:, :, 0:64], in1=den[k][:].unsqueeze(2).to_broadcast([128, 4, 64]), op=ALU.mult),
                      reads=[B_bank[4 + b], B_den[k]], writes=[B_ar[2 * st + b // 2]])

        prevu = None
        for ui in range(len(units)):
            p = s_unit(ui)
            if prevu is not None:
                av_unit(*prevu)
            prevu = (ui, p)
        av_unit(*prevu)
        if mt < 3:
            S.add("pool", lambda e: e.tensor_copy(out=K1T[:, :, 0:128], in_=K1T[:, :, 512:640]), reads=B_K1T, writes=B_K1Tprev)
            S.add("pool", lambda e: e.tensor_copy(out=V1[:, 0, :, 0:64], in_=V1[:, 4, :, 0:64]), reads=[B_V1[4]], writes=[B_V1[0]])
        transposes_to(actA, B_actA, None)
        r = out_proj("wo1", actA, B_actA)
        if hook is not None:
            hook[0]()
        ffn(1, P_FN1, r, hook[1] if hook is not None else None, store_g0=store_g0)

    nmt = S_LEN // T if n_mt is None else n_mt
    for seq in range(nseq):
        for mt in range(nmt):
            g0 = seq * S_LEN + mt * T
            ti_ = seq * nmt + mt
            if ti_ == 0:
                issue_loads(g0)
                trig_tables(seq, mt * T)
            r0 = load_x(g0)
            hook = None
            if ti_ + 1 < nseq * nmt:
                nseq_, nmt_ = divmod(ti_ + 1, nmt)
                hook = ((lambda g1=nseq_ * S_LEN + nmt_ * T: issue_loads(g1)),
                        ((lambda a=nseq_, b=nmt_: trig_dve(a, b * T)), trig_act))
            r1 = r0
            if 0 in layers:
                r1 = layer0(mt, hook if 1 not in layers else None, r0, g0 if 1 not in layers else None)
            if 1 in layers:
                layer1(mt, hook, r1, g0)
            SS.tile += 1
    S.add("sp", None, reads=B_out)
    S.emit()
    es.close()
    return nc, S.stats + (SBUF_LEFT,)


def host_pack(inputs):
    f = np.float32
    prm = np.zeros((128, NPC), f)

    def pc(v):
        return np.ascontiguousarray(np.asarray(v, f).reshape(8, 128).T)

    prm[:, P_AN0:P_AN0 + 8] = pc(inputs["attn_norm"][0])
    prm[:, P_AN1:P_AN1 + 8] = pc(inputs["attn_norm"][1])
    prm[:, P_FN0:P_FN0 + 8] = pc(inputs["ffn_norm"][0])
    prm[:, P_FN1:P_FN1 + 8] = pc(inputs["ffn_norm"][1])
    prm[:, P_KVN:P_KVN + 8] = pc(inputs["kv_norm"])
    prm[:, P_QN0] = np.asarray(inputs["da_q_norm"], f)[0].reshape(128)
    prm[:, P_KN0] = np.asarray(inputs["da_k_norm"], f)[0].reshape(128)
    prm[:, P_SUB] = np.asarray(inputs["da_subln"], f)[0]
    prm[:, P_KN1] = np.tile(np.asarray(inputs["k_norm"], f), 2)
    prm[:, P_QN1] = np.tile(np.asarray(inputs["sw_q_norm"], f)[0], 2)
    prm[:, P_SINK:P_SINK + 16] = np.asarray(inputs["sw_sinks"], f)[0][None, :]
    prm[:, P_LAM:P_LAM + 256] = np.asarray(inputs["da_lambda"], f)[0].reshape(1, 256)
    inv = (1.0 / (np.float32(10000.0) ** (np.arange(0, 64, 2, dtype=f) / np.float32(64)))).astype(f)
    prm[:, P_INVF] = np.tile(inv, 4)
    c = np.zeros((128, NCC), f)
    ar = np.arange(128)
    c[:, C_ID:C_ID + 128] = np.eye(128, dtype=f)
    c[:, C_MCUR:C_MCUR + 128] = np.where(ar[:, None] <= ar[None, :], 0.0, NEG)
    c[:, C_MPREV:C_MPREV + 128] = np.where(ar[:, None] > ar[None, :], 0.0, NEG)
    c[:, C_BONES:C_BONES + 128] = np.where((ar[:, None] // 64) == (ar[None, :] // 64), 1.0 / 64.0, 0.0)
    c[:, C_OMEAN:C_OMEAN + 128] = 1.0 / 1024.0
    R = np.zeros((128, 128), f)
    for m in range(128):
        if (m % 64) < 32:
            R[m + 32, m] = -1.0
        else:
            R[m - 32, m] = 1.0
    c[:, C_RPERM:C_RPERM + 128] = R
    return prm, c


_PROG = {}


def kernel(**inputs):
    n_cores = 8
    nseq = 2
    if "p" not in _PROG:
        _PROG["p"] = build_program(nseq=nseq)[0]
    nc = _PROG["p"]
    prm, c = host_pack(inputs)
    x = np.ascontiguousarray(np.asarray(inputs["x"], np.float32))
    pos = np.ascontiguousarray(np.asarray(inputs["positions"], np.int32))
    shared = {k: np.ascontiguousarray(np.asarray(inputs[k], np.float32)) for k in
              ("da_w_qkv", "da_w_o", "w_gate_up", "w_down", "w_kv", "sw_w_q", "sw_w_o")}
    in_maps = []
    for i in range(n_cores):
        m = dict(shared)
        m["x"] = x[i * nseq:(i + 1) * nseq].reshape(nseq * S_LEN, D)
        m["pos"] = pos[i * nseq:(i + 1) * nseq]
        m["params"] = prm
        m["consts"] = c
        in_maps.append(m)
    res = run_bass_kernel_spmd(nc, in_maps, core_ids=list(range(n_cores)))
    out = np.concatenate([r["out"].reshape(nseq, S_LEN, D) for r in res.results], axis=0)
    return out.astype(np.float32)
```
